# Optimizing a Trainium2 kernel written in Bass

```python
import math
import jax, jax.numpy as jnp
from jax import lax
import numpy as np

D_MODEL = 1024
BATCH = 8
SEQ = 2048
DEPTH = 2
DEC_BATCH = 4
DEC_SEQ = 8192
PAST_LEN = 128

N_META = 16
MLA_HEADS = 16
QK_NOPE = 64
QK_ROPE = 32
V_HEAD = 64
Q_LORA = 384
KV_LORA = 256
ATT_W = MLA_HEADS * V_HEAD
ROPE_THETA = 10000.0
Q_BLOCK = 128
SSM_EXPAND = 2
D_INNER = SSM_EXPAND * D_MODEL
SSM_HEADDIM = 64
SSM_HEADS = D_INNER // SSM_HEADDIM
SSM_GROUPS = 4
D_STATE = 128
D_CONV = 5
CONV_CH = D_INNER + 2 * SSM_GROUPS * D_STATE
CHUNK = 128
D_FF = 2816
IN_SPLITS = (Q_LORA, KV_LORA, QK_ROPE, D_INNER, CONV_CH, 2 * SSM_HEADS, 2 * D_MODEL)
IN_W = sum(IN_SPLITS)
EPS = 1e-6

kernel_name = 'hybrid_mla_ssd_macaron_encoder'


def rmsnorm(x, g):
    x32 = x.astype(jnp.float32)
    y = x32 * lax.rsqrt(jnp.mean(x32 * x32, axis=-1, keepdims=True) + EPS)
    return (y * g.astype(jnp.float32)).astype(x.dtype)


def swiglu(x, w_in, w_out):
    gu = x @ w_in
    g, u = jnp.split(gu, 2, axis=-1)
    return (jax.nn.silu(g) * u) @ w_out


def split_cols(x, sizes):
    offs = np.cumsum(sizes)[:-1].tolist()
    return jnp.split(x, offs, axis=-1)


def rope_tables(length, dim):
    pos = jnp.arange(length, dtype=jnp.float32)
    inv = ROPE_THETA ** (-jnp.arange(0, dim, 2, dtype=jnp.float32) / dim)
    ang = pos[:, None] * inv[None, :]
    ang = jnp.concatenate([ang, ang], axis=-1)
    return jnp.cos(ang), jnp.sin(ang)


def apply_rope(x, cos, sin):
    x32 = x.astype(jnp.float32)
    x1, x2 = jnp.split(x32, 2, axis=-1)
    rot = jnp.concatenate([-x2, x1], axis=-1)
    return (x32 * cos + rot * sin).astype(x.dtype)


def block_attention(q, k, v):
    b_, L, h, dq = q.shape
    nb = -(-L // Q_BLOCK)
    lp = nb * Q_BLOCK
    qp = jnp.pad(q, ((0, 0), (0, lp - L), (0, 0), (0, 0)))
    qb = jnp.moveaxis(qp.reshape(b_, nb, Q_BLOCK, h, dq), 1, 0)
    scale = dq ** -0.5

    def one(qblk):
        s = jnp.einsum('bqhd,bkhd->bhqk', qblk, k).astype(jnp.float32) * scale
        p = jax.nn.softmax(s, axis=-1).astype(v.dtype)
        return jnp.einsum('bhqk,bkhd->bqhd', p, v)

    o = lax.map(one, qb)
    return jnp.moveaxis(o, 0, 1).reshape(b_, lp, h, v.shape[-1])[:, :L]


def ssd_chunked(x, dt, a, bm, cm):
    b_, T, h, p = x.shape
    g, n = bm.shape[2], bm.shape[3]
    j = h // g
    c = T // CHUNK
    xr = (x.astype(jnp.float32) * dt[..., None]).reshape(b_, c, CHUNK, g, j, p)
    da = (dt * a.astype(jnp.float32)).reshape(b_, c, CHUNK, g, j)
    a_cs = jnp.cumsum(da, axis=2)
    bc = bm.astype(jnp.float32).reshape(b_, c, CHUNK, g, n)
    cc = cm.astype(jnp.float32).reshape(b_, c, CHUNK, g, n)
    idx = jnp.arange(CHUNK)
    lower = (idx[:, None] >= idx[None, :])[None, None, :, :, None, None]
    seg = a_cs[:, :, :, None] - a_cs[:, :, None, :]
    lmat = jnp.exp(jnp.where(lower, seg, -jnp.inf))
    cb = jnp.einsum('bclgn,bcsgn->bclsg', cc, bc)
    y_diag = jnp.einsum('bclsg,bclsgj,bcsgjp->bclgjp', cb, lmat, xr)
    decay_states = jnp.exp(a_cs[:, :, -1:] - a_cs)
    states = jnp.einsum('bcsgn,bcsgj,bcsgjp->bcgjpn', bc, decay_states, xr)
    chunk_decay = jnp.exp(a_cs[:, :, -1])

    def step(s, inp):
        st, dec = inp
        return s * dec[..., None, None] + st, s

    init = jnp.zeros((b_, g, j, p, n), jnp.float32)
    _, prev = lax.scan(step, init, (jnp.moveaxis(states, 1, 0), jnp.moveaxis(chunk_decay, 1, 0)))
    prev = jnp.moveaxis(prev, 0, 1)
    y_off = jnp.einsum('bclgn,bcgjpn,bclgj->bclgjp', cc, prev, jnp.exp(a_cs))
    return (y_diag + y_off).reshape(b_, T, h, p).astype(x.dtype)


def mamba_branch(z, xbc, dt_raw, conv_w, conv_b, a_log, dt_bias, d_skip, ssm_norm):
    b_, L, _ = xbc.shape
    xbc = lax.conv_general_dilated(xbc, conv_w[:, None, :].astype(xbc.dtype), window_strides=(1,),
                                   padding=[(D_CONV // 2, D_CONV // 2)],
                                   dimension_numbers=('NWC', 'WIO', 'NWC'),
                                   feature_group_count=CONV_CH)
    xbc = jax.nn.silu(xbc + conv_b)
    xs, bm, cm = split_cols(xbc, (D_INNER, SSM_GROUPS * D_STATE, SSM_GROUPS * D_STATE))
    xs = xs.reshape(b_, L, SSM_HEADS, SSM_HEADDIM)
    bm = bm.reshape(b_, L, SSM_GROUPS, D_STATE)
    cm = cm.reshape(b_, L, SSM_GROUPS, D_STATE)
    dt = jax.nn.softplus(dt_raw.astype(jnp.float32).reshape(b_, L, 2, SSM_HEADS)
                         + dt_bias.astype(jnp.float32))
    a = -jnp.exp(a_log.astype(jnp.float32))
    pad = (-L) % CHUNK
    padt = lambda t: jnp.pad(t, ((0, 0), (pad, 0)) + ((0, 0),) * (t.ndim - 2))
    xs_p, bm_p, cm_p, dt_p = padt(xs), padt(bm), padt(cm), padt(dt)
    y_f = ssd_chunked(xs_p, dt_p[:, :, 0], a[0], bm_p, cm_p)
    fl = lambda t: jnp.flip(t, axis=1)
    y_b = fl(ssd_chunked(fl(xs_p), fl(dt_p[:, :, 1]), a[1], fl(bm_p), fl(cm_p)))
    y = (y_f + y_b)[:, pad:] + xs * d_skip[:, None].astype(xs.dtype)
    y = y.reshape(b_, L, D_INNER) * jax.nn.silu(z)
    yg = y.reshape(b_, L, SSM_GROUPS, D_INNER // SSM_GROUPS).astype(jnp.float32)
    yg = yg * lax.rsqrt(jnp.mean(yg * yg, axis=-1, keepdims=True) + EPS)
    return (yg.reshape(b_, L, D_INNER) * ssm_norm.astype(jnp.float32)).astype(z.dtype)


def hybrid_mixer(h, w_in, q_norm, w_uq, kv_norm, w_ukv, conv_w, conv_b, a_log, dt_bias, d_skip,
                 ssm_norm, w_branch, w_out):
    b_, L, _ = h.shape
    proj = h @ w_in
    cq, ckv, kr, z, xbc, dt_raw, gate_logits = split_cols(proj, IN_SPLITS)
    cos, sin = rope_tables(L, QK_ROPE)
    q = (rmsnorm(cq, q_norm) @ w_uq).reshape(b_, L, MLA_HEADS, QK_NOPE + QK_ROPE)
    q_nope, q_rope = q[..., :QK_NOPE], q[..., QK_NOPE:]
    q_rope = apply_rope(q_rope, cos[None, :, None, :], sin[None, :, None, :])
    kv = (rmsnorm(ckv, kv_norm) @ w_ukv).reshape(b_, L, MLA_HEADS, QK_NOPE + V_HEAD)
    k_nope, v = kv[..., :QK_NOPE], kv[..., QK_NOPE:]
    k_rope = apply_rope(kr, cos[None], sin[None])
    k_rope = jnp.broadcast_to(k_rope[:, :, None, :], (b_, L, MLA_HEADS, QK_ROPE))
    qf = jnp.concatenate([q_nope, q_rope], axis=-1)
    kf = jnp.concatenate([k_nope, k_rope], axis=-1)
    o_a = block_attention(qf, kf, v).reshape(b_, L, ATT_W)
    o_m = mamba_branch(z, xbc, dt_raw, conv_w, conv_b, a_log, dt_bias, d_skip, ssm_norm)
    y_a = o_a @ w_branch[:ATT_W]
    y_m = o_m @ w_branch[ATT_W:]
    gates = jax.nn.sigmoid(gate_logits.astype(jnp.float32)).reshape(b_, L, 2, D_MODEL).astype(h.dtype)
    mix = gates[:, :, 0] * y_a + gates[:, :, 1] * y_m
    return mix @ w_out


def trunk(x, meta_tokens, ffn1_norm, ffn1_w_in, ffn1_w_out, mix_norm, w_in, q_norm, w_uq, kv_norm,
          w_ukv, conv_w, conv_b, a_log, dt_bias, d_skip, ssm_norm, w_branch, w_out, ffn2_norm,
          ffn2_w_in, ffn2_w_out, final_norm):
    b_ = x.shape[0]
    meta = jnp.broadcast_to(meta_tokens[None].astype(x.dtype), (b_, N_META, D_MODEL))
    x = jnp.concatenate([meta, x], axis=1)
    for l in range(DEPTH):
        x = x + 0.5 * swiglu(rmsnorm(x, ffn1_norm[l]), ffn1_w_in[l], ffn1_w_out[l])
        h = rmsnorm(x, mix_norm[l])
        x = x + hybrid_mixer(h, w_in[l], q_norm[l], w_uq[l], kv_norm[l], w_ukv[l], conv_w[l], conv_b[l],
                             a_log[l], dt_bias[l], d_skip[l], ssm_norm[l], w_branch[l], w_out[l])
        x = x + 0.5 * swiglu(rmsnorm(x, ffn2_norm[l]), ffn2_w_in[l], ffn2_w_out[l])
    x = rmsnorm(x, final_norm)
    return x[:, N_META:]


def setup_inputs(seed: int = 0) -> dict:
    key = jax.random.key(seed)
    ks = jax.random.split(key, 32)
    nrm = lambda k, shape, s: jax.random.normal(k, shape, jnp.float32) * s
    gain = lambda k, shape: 1.0 + 0.02 * jax.random.normal(k, shape, jnp.float32)
    dt0 = jnp.exp(jax.random.uniform(ks[20], (DEPTH, 2, SSM_HEADS), jnp.float32,
                                     math.log(1e-3), math.log(1e-1)))
    w_branch = jnp.concatenate([nrm(ks[21], (DEPTH, ATT_W, D_MODEL), ATT_W ** -0.5),
                                nrm(ks[22], (DEPTH, D_INNER, D_MODEL), D_INNER ** -0.5)], axis=1)
    return {
        'x_prompt': nrm(ks[0], (BATCH, SEQ, D_MODEL), 1.0),
        'x_sample': nrm(ks[1], (DEC_BATCH, DEC_SEQ, D_MODEL), 1.0),
        'meta_tokens': nrm(ks[2], (N_META, D_MODEL), 1.0),
        'ffn1_norm': gain(ks[3], (DEPTH, D_MODEL)),
        'ffn1_w_in': nrm(ks[4], (DEPTH, D_MODEL, 2 * D_FF), D_MODEL ** -0.5),
        'ffn1_w_out': nrm(ks[5], (DEPTH, D_FF, D_MODEL), D_FF ** -0.5),
        'mix_norm': gain(ks[6], (DEPTH, D_MODEL)),
        'w_in': nrm(ks[7], (DEPTH, D_MODEL, IN_W), D_MODEL ** -0.5),
        'q_norm': gain(ks[8], (DEPTH, Q_LORA)),
        'w_uq': nrm(ks[9], (DEPTH, Q_LORA, MLA_HEADS * (QK_NOPE + QK_ROPE)), Q_LORA ** -0.5),
        'kv_norm': gain(ks[10], (DEPTH, KV_LORA)),
        'w_ukv': nrm(ks[11], (DEPTH, KV_LORA, MLA_HEADS * (QK_NOPE + V_HEAD)), KV_LORA ** -0.5),
        'conv_w': nrm(ks[12], (DEPTH, D_CONV, CONV_CH), D_CONV ** -0.5),
        'conv_b': nrm(ks[13], (DEPTH, CONV_CH), 0.02),
        'a_log': jnp.log(jax.random.uniform(ks[14], (DEPTH, 2, SSM_HEADS), jnp.float32, 1.0, 16.0)),
        'dt_bias': dt0 + jnp.log(-jnp.expm1(-dt0)),
        'd_skip': gain(ks[15], (DEPTH, SSM_HEADS)),
        'ssm_norm': gain(ks[16], (DEPTH, D_INNER)),
        'w_branch': w_branch,
        'w_out': nrm(ks[17], (DEPTH, D_MODEL, D_MODEL), D_MODEL ** -0.5),
        'ffn2_norm': gain(ks[18], (DEPTH, D_MODEL)),
        'ffn2_w_in': nrm(ks[19], (DEPTH, D_MODEL, 2 * D_FF), D_MODEL ** -0.5),
        'ffn2_w_out': nrm(ks[23], (DEPTH, D_FF, D_MODEL), D_FF ** -0.5),
        'final_norm': gain(ks[24], (D_MODEL,)),
    }


def reference(x_prompt, x_sample, meta_tokens, ffn1_norm, ffn1_w_in, ffn1_w_out, mix_norm, w_in, q_norm,
              w_uq, kv_norm, w_ukv, conv_w, conv_b, a_log, dt_bias, d_skip, ssm_norm, w_branch, w_out,
              ffn2_norm, ffn2_w_in, ffn2_w_out, final_norm):
    y_prompt = trunk(x_prompt, meta_tokens, ffn1_norm, ffn1_w_in, ffn1_w_out, mix_norm, w_in, q_norm, w_uq,
                     kv_norm, w_ukv, conv_w, conv_b, a_log, dt_bias, d_skip, ssm_norm, w_branch, w_out,
                     ffn2_norm, ffn2_w_in, ffn2_w_out, final_norm)
    y_sample = trunk(x_sample, meta_tokens, ffn1_norm, ffn1_w_in, ffn1_w_out, mix_norm, w_in, q_norm, w_uq,
                     kv_norm, w_ukv, conv_w, conv_b, a_log, dt_bias, d_skip, ssm_norm, w_branch, w_out,
                     ffn2_norm, ffn2_w_in, ffn2_w_out, final_norm)
    return (y_prompt, y_sample)
```

```python
import contextlib
import numpy as np
import concourse.bass as bass
import concourse.mybir as mybir
from concourse.bass_utils import run_bass_kernel_spmd

F32 = mybir.dt.float32
BF16 = mybir.dt.bfloat16
AF = mybir.ActivationFunctionType
ALU = mybir.AluOpType
AX = mybir.AxisListType

D = 1024
DEPTH = 2
N_META = 16
HEADS = 16
QK_NOPE = 64
QK_ROPE = 32
V_HEAD = 64
Q_LORA = 384
KV_LORA = 256
D_INNER = 2048
SSM_HEADS = 32
SSM_HD = 64
SSM_GROUPS = 4
D_STATE = 128
D_CONV = 5
CONV_CH = 3072
D_FF = 2816
IN_W = 7904
EPS = 1e-6
O_CQ = 0
O_CKV = 384
O_KR = 640
O_Z = 672
O_XBC = 2720
O_DT = 5792
O_GATE = 5856

ENGS = ("pe", "act", "dve", "pool", "sp")


class Op:
    __slots__ = ("eng", "fn", "dma", "deps", "idx", "signal", "sigval", "dsem", "dval", "dprev")

    def __init__(self, eng, fn, dma):
        self.eng = eng
        self.fn = fn
        self.dma = dma
        self.deps = []
        self.signal = False
        self.sigval = 0
        self.dsem = None
        self.dval = 0
        self.dprev = 0


class Sched:
    def __init__(self, nc, n_dma_sems=(("sp", 44), ("pool", 24), ("act", 16))):
        self.nc = nc
        self.ops = {e: [] for e in ENGS}
        self.lastw = {}
        self.readers = {}
        self.n_dma_sems = dict(n_dma_sems)
        self._bar_mark = {e: 0 for e in ENGS}

    def add(self, eng, fn, reads=(), writes=(), dma=False):
        op = Op(eng, fn, dma)
        deps = set()
        for r in reads:
            w = self.lastw.get(r)
            if w is not None:
                deps.add(w)
        for r in writes:
            w = self.lastw.get(r)
            if w is not None:
                deps.add(w)
            for rd in self.readers.get(r, ()):
                deps.add(rd)
        for r in reads:
            self.readers.setdefault(r, []).append(op)
        for r in writes:
            self.lastw[r] = op
            self.readers[r] = []
        op.deps = [d for d in deps if not (d.eng == "pe" and eng == "pe" and not d.dma and not dma)]
        op.idx = len(self.ops[eng])
        self.ops[eng].append(op)
        return op

    def barrier(self):
        lasts = []
        for e in ENGS:
            comp = [o for o in self.ops[e] if not o.dma and o.fn is not None]
            if comp:
                lasts.append(comp[-1])
            lasts.extend(o for o in self.ops[e][self._bar_mark[e]:] if o.dma)
        for e in ENGS:
            op = Op(e, None, False)
            op.deps = list(lasts)
            op.idx = len(self.ops[e])
            self.ops[e].append(op)
        self.lastw = {}
        self.readers = {}
        self._bar_mark = {e: len(self.ops[e]) for e in ENGS}

    def emit(self):
        nc = self.nc
        for e in ENGS:
            for op in self.ops[e]:
                for d in op.deps:
                    d.signal = True
        with contextlib.ExitStack() as st:
            esem = {e: st.enter_context(nc.semaphore(f"s_{e}")) for e in ENGS}
            dsems = {e: [st.enter_context(nc.semaphore(f"d_{e}{i}")) for i in range(n)]
                     for e, n in self.n_dma_sems.items()}
            for e in ENGS:
                c = 0
                j = 0
                for op in self.ops[e]:
                    if op.dma:
                        pool = dsems[e]
                        op.dsem = pool[j % len(pool)]
                        op.dval = 16 * (j // len(pool) + 1)
                        op.dprev = 16 * (j // len(pool))
                        j += 1
                    elif op.signal:
                        c += 1
                        op.sigval = c
            block = st.enter_context(nc.Block())

            def run(e, eng):
                waited = {}

                def wait(sem, val):
                    k = id(sem)
                    if waited.get(k, 0) >= val:
                        return
                    waited[k] = val
                    eng.wait_ge(sem, val)

                for op in self.ops[e]:
                    for d in op.deps:
                        if d.dma:
                            wait(d.dsem, d.dval)
                        else:
                            wait(esem[d.eng], d.sigval)
                    if op.dma:
                        if op.dprev > 0:
                            wait(op.dsem, op.dprev)
                        op.fn(eng).then_inc(op.dsem, 16)
                    elif op.fn is None:
                        if op.signal:
                            eng.nop().then_inc(esem[e], 1)
                    else:
                        ins = op.fn(eng)
                        if op.signal:
                            ins.then_inc(esem[e], 1)

            block.tensor(lambda eng: run("pe", eng))
            block.scalar(lambda eng: run("act", eng))
            block.vector(lambda eng: run("dve", eng))
            block.gpsimd(lambda eng: run("pool", eng))
            block.sync(lambda eng: run("sp", eng))


class Arena:
    def __init__(self, ap, words):
        self.ap = ap
        self.words = words
        self.base = 0
        self.cur = 0

    def alloc(self, free_shape, dtype=F32):
        n = int(np.prod(free_shape))
        w = n if dtype == F32 else (n + 1) // 2
        w = (w + 7) // 8 * 8
        assert self.cur + w <= self.words, (self.cur, w, self.words)
        v = self.ap[:, self.cur:self.cur + w]
        self.cur += w
        if dtype != F32:
            v = v.bitcast(dtype)
        v = v[:, 0:n]
        if len(free_shape) == 2:
            v = v.rearrange("p (a b) -> p a b", a=free_shape[0])
        elif len(free_shape) == 3:
            v = v.rearrange("p (a b c) -> p a b c", a=free_shape[0], b=free_shape[1])
        return v

    def mark_persistent(self):
        self.base = self.cur

    def reset(self):
        self.cur = self.base


def seq_tiles(L, T):
    return [(0, N_META)] + [(N_META + T * i, T) for i in range((L - N_META) // T)]


class Prog:
    def __init__(self, Ls, T=256, stop_after=None, dbg=()):
        self.Ls = list(Ls)
        self.T = T
        self.offs = [int(x) for x in np.cumsum([0] + self.Ls[:-1])]
        self.Ltot = int(sum(self.Ls))
        self.stop_after = stop_after
        self.dbg = dbg
        self.uid = 0

    def k(self, *a):
        return a

    def dma(self, q, out, in_, reads=(), writes=(), **kw):
        return self.S.add(q, lambda e: e.dma_start(out=out, in_=in_, **kw), reads, writes, dma=True)

    def mm(self, out, lhsT, rhs, start, stop, reads, writes):
        return self.S.add("pe", lambda e: e.matmul(out, lhsT=lhsT, rhs=rhs, start=start, stop=stop), reads, writes)

    def tr(self, out, in_, ident, reads, writes):
        return self.S.add("pe", lambda e: e.transpose(out=out, in_=in_, identity=ident), reads, writes)

    def act(self, out, in_, func, reads, writes, eng="act", **kw):
        return self.S.add("act", lambda e: e.activation(out=out, in_=in_, func=func, **kw), reads, writes)

    def tt(self, eng, out, in0, in1, op, reads, writes):
        return self.S.add(eng, lambda e: e.tensor_tensor(out=out, in0=in0, in1=in1, op=op), reads, writes)

    def ts(self, eng, out, in0, s1, s2, op0, op1, reads, writes):
        if op1 is None:
            return self.S.add(eng, lambda e: e.tensor_scalar(out=out, in0=in0, scalar1=s1, scalar2=None, op0=op0), reads, writes)
        return self.S.add(eng, lambda e: e.tensor_scalar(out=out, in0=in0, scalar1=s1, scalar2=s2, op0=op0, op1=op1), reads, writes)

    def stt(self, eng, out, in0, scalar, in1, op0, op1, reads, writes):
        return self.S.add(eng, lambda e: e.scalar_tensor_tensor(out=out, in0=in0, scalar=scalar, in1=in1, op0=op0, op1=op1), reads, writes)

    def cp(self, eng, out, in_, reads, writes):
        if eng == "act":
            return self.S.add("act", lambda e: e.copy(out=out, in_=in_), reads, writes)
        return self.S.add(eng, lambda e: e.tensor_copy(out=out, in_=in_), reads, writes)

    def memset(self, eng, ap, val, writes):
        return self.S.add(eng, lambda e: e.memset(ap, val), (), writes)

    def psum(self):
        i = self.ps_i
        self.ps_i = (i + 1) % 8
        return self.ps[i], ("ps", i)

    def build(self):
        nc = bass.Bass("TRN2", target_bir_lowering=False)
        self.nc = nc
        Ls, Ltot = self.Ls, self.Ltot
        dr = {}

        def din(name, shape):
            dr[name] = nc.dram_tensor(name, list(shape), F32, kind="ExternalInput").ap()

        for i, L in enumerate(Ls):
            din(f"x{i}", (L - N_META, D))
        din("meta_tokens", (N_META, D))
        din("ffn1_norm", (DEPTH, D))
        din("ffn1_w_in", (DEPTH, D, 2 * D_FF))
        din("ffn1_w_out", (DEPTH, D_FF, D))
        din("mix_norm", (DEPTH, D))
        din("w_in", (DEPTH, D, IN_W))
        din("q_norm", (DEPTH, Q_LORA))
        din("w_uq", (DEPTH, Q_LORA, HEADS * 96))
        din("kv_norm", (DEPTH, KV_LORA))
        din("w_ukv", (DEPTH, KV_LORA, HEADS * 128))
        din("conv_w", (DEPTH, D_CONV, CONV_CH))
        din("conv_b", (DEPTH, CONV_CH))
        din("a_log", (DEPTH, 2, SSM_HEADS))
        din("dt_bias", (DEPTH, 2, SSM_HEADS))
        din("d_skip", (DEPTH, SSM_HEADS))
        din("ssm_norm", (DEPTH, D_INNER))
        din("w_branch", (DEPTH, 3072, D))
        din("w_out", (DEPTH, D, D))
        din("ffn2_norm", (DEPTH, D))
        din("ffn2_w_in", (DEPTH, D, 2 * D_FF))
        din("ffn2_w_out", (DEPTH, D_FF, D))
        din("final_norm", (D,))
        for i, L in enumerate(Ls):
            dr[f"y{i}"] = nc.dram_tensor(f"y{i}", [L - N_META, D], F32, kind="ExternalOutput").ap()
        self.dr = dr

        def scratch(name, shape, dtype):
            kind = "ExternalOutput" if name in self.dbg else "Internal"
            dr[name] = nc.dram_tensor(name, list(shape), dtype, kind=kind).ap()

        scratch("XT", (8, 128, Ltot), F32)
        scratch("KR", (32, Ltot), BF16)
        scratch("QT", (HEADS, 96, Ltot), BF16)
        scratch("KT", (HEADS, 64, Ltot), BF16)
        scratch("V", (Ltot, HEADS * 65), BF16)
        scratch("SZ", (Ltot, D_INNER), BF16)
        scratch("DT", (Ltot, 64), F32)
        scratch("XBC", (24, 128, Ltot), BF16)
        scratch("GT", (16, 128, Ltot), BF16)
        scratch("OA", (8, 128, Ltot), BF16)
        scratch("OM", (16, 128, Ltot), BF16)
        scratch("XS", (Ltot, D_INNER), BF16)
        scratch("BTOK", (Ltot, 512), BF16)
        scratch("BT", (4, 128, Ltot), BF16)
        scratch("CT", (4, 128, Ltot), BF16)
        scratch("YB", (Ltot, D_INNER), F32)
        self.scratch = scratch

        with contextlib.ExitStack() as st:
            AW = 51200
            arena_t = st.enter_context(nc.sbuf_tensor("arena", [128, AW], F32))
            self.A = Arena(arena_t, AW)
            self.ps = [st.enter_context(nc.psum_tensor(f"ps{i}", [128, 512], F32)) for i in range(8)]
            self.ps_i = 0
            self.S = Sched(nc)
            self.consts()
            self.S.barrier()
            self.body()
            self.S.barrier()
            self.S.emit()
        return nc

    def consts(self):
        A = self.A
        ones_f = A.alloc([128])
        self.ident = A.alloc([128])
        self.identb = A.alloc([128], BF16)
        self.onesb = A.alloc([128], BF16)
        self.memset("pool", ones_f, 1.0, ["c_onesf"])
        self.S.add("pool", lambda e: e.affine_select(out=self.ident, in_=ones_f, pattern=[[-1, 128]], compare_op=ALU.is_equal,
                                                      fill=0.0, base=0, channel_multiplier=1), ["c_onesf"], ["c_ident"])
        self.cp("dve", self.identb, self.ident, ["c_ident"], ["c_identb"])
        self.cp("dve", self.onesb, ones_f, ["c_onesf"], ["c_onesb"])
        self.ones_f = ones_f
        A.mark_persistent()

    def body(self):
        for l in range(DEPTH):
            self.ffn_phase(l, 1)
            if self.stop_after == ("ffn1", l):
                break
            self.proj1_phase(l)
            if self.stop_after == ("proj1", l):
                break
            self.proj2_phase(l)
            if self.stop_after == ("proj2", l):
                break
            self.attn_phase(l)
            if self.stop_after == ("attn", l):
                break
            self.ssd_phase(l)
            if self.stop_after == ("ssd", l):
                break
            self.merge_phase(l)
            if self.stop_after == ("merge", l):
                break
            self.ffn_phase(l, 2)
        self.out_phase()

    def load_xT(self, first, si, t0, n, xT, key, scr):
        g0 = self.offs[si] + t0
        if not first:
            self.dma("sp", xT[:, :, 0:n], self.dr["XT"][:, :, g0:g0 + n].rearrange("c p t -> p c t"),
                     [("XT", si, t0)], [key])
            return
        xin, kin = scr
        nb = max(1, n // 128)
        if t0 == 0:
            self.dma("sp", xin[0:n, 0, :], self.dr["meta_tokens"][0:n, :], (), [kin])
        else:
            src = self.dr[f"x{si}"][t0 - N_META:t0 - N_META + n, :].rearrange("(b p) d -> p b d", p=128)
            self.dma("sp", xin[:, 0:nb, :], src, (), [kin])
        pn = min(n, 128)
        for c in range(8):
            ps, pk = self.psum()
            for b in range(nb):
                self.tr(ps[:, b * 128:b * 128 + pn], xin[0:pn, b, c * 128:(c + 1) * 128], self.ident[0:pn, 0:pn],
                        [kin, "c_ident"], [pk])
            self.cp("act" if c % 2 == 0 else "dve", xT[:, c, 0:n], ps[:, 0:n], [pk], [key])

    def store_xT(self, si, t0, n, xT, kx):
        g0 = self.offs[si] + t0
        self.dma("sp", self.dr["XT"][:, :, g0:g0 + n].rearrange("c p t -> p c t"), xT[:, :, 0:n], [kx], [("XT", si, t0)])

    def tile_loop(self, tiles, load, compute):
        if tiles:
            load(0, tiles[0])
        for i, tl in enumerate(tiles):
            if i + 1 < len(tiles):
                load(i + 1, tiles[i + 1])
            compute(i, tl)

    def rms_rstd(self, src, nch, n, rstd, kr, sq, ksq, reads, dim):
        for c in range(nch):
            self.act(sq[:, c, 0:n], src[:, c, 0:n], AF.Square, reads, [ksq + (c,)])
        ps, pk = self.psum()
        for c in range(nch):
            self.mm(ps[:, 0:n], self.onesb, sq[:, c, 0:n], c == 0, c == nch - 1, [ksq + (c,), "c_onesb"], [pk])
        self.act(rstd[:, 0:n], ps[:, 0:n], AF.Sqrt, [pk], [kr], scale=1.0 / dim, bias=self.eps_col)
        self.S.add("dve", lambda e: e.reciprocal(out=rstd[:, 0:n], in_=rstd[:, 0:n]), [kr], [kr])

    def load_w_bf16(self, dst, src2d, nchunk, key, col0=0, ncol=None):
        ncol = ncol if ncol is not None else src2d.shape[1]
        for c in range(nchunk):
            self.dma("pool", dst[:, c, :], src2d[c * 128:(c + 1) * 128, col0:col0 + ncol], (), [key + (c,)],
                     max_dma_last_dim=4096)

    def load_col(self, dst, vec, nch, key):
        self.dma("sp", dst, vec.rearrange("(c p) -> p c", p=128), (), [key], allow_slow_non_contiguous=True)

    def ffn_phase(self, l, which):
        A, T = self.A, self.T
        A.reset()
        dr = self.dr
        pre = f"ffn{which}"
        first = (l == 0 and which == 1)
        last = (l == DEPTH - 1 and which == 2)
        w_in = A.alloc([8, 2 * D_FF], BF16)
        w_out = A.alloc([22, D], BF16)
        gcol = A.alloc([8])
        self.eps_col = A.alloc([1])
        self.memset("pool", self.eps_col, EPS, ["eps"])
        self.load_col(gcol, dr[f"{pre}_norm"][l], 8, ("gcol",))
        self.load_w_bf16(w_in, dr[f"{pre}_w_in"][l], 8, ("w_in",))
        self.load_w_bf16(w_out, dr[f"{pre}_w_out"][l], 22, ("w_out",))
        xTs = [A.alloc([8, T]) for _ in range(2)]
        hs = [A.alloc([8, T], BF16) for _ in range(2)]
        sq = A.alloc([8, T], BF16)
        rstd = A.alloc([T])
        aT = A.alloc([22, T], BF16)
        sg = [A.alloc([T]) for _ in range(2)]
        xin = A.alloc([T // 128, D]) if first else None
        tiles = [(si, t0, n) for si, L in enumerate(self.Ls) for (t0, n) in seq_tiles(L, T)]

        def load(i, tl):
            si, t0, n = tl
            self.load_xT(first, si, t0, n, xTs[i % 2], ("xT", i % 2), (xin, ("xin",)))

        def compute(i, tl):
            si, t0, n = tl
            s = i % 2
            xT, h = xTs[s], hs[s]
            kx = ("xT", s)
            self.rms_rstd(xT, 8, n, rstd, ("rstd",), sq, ("sq",), [kx], D)
            for c in range(8):
                self.stt("dve", h[:, c, 0:n], xT[:, c, 0:n], gcol[:, c:c + 1], rstd[:, 0:n], ALU.mult, ALU.mult,
                         [kx, ("rstd",), ("gcol",)], [("h", s, c)])
            for j in range(22):
                pg, kg = self.psum()
                pu, ku = self.psum()
                for c in range(8):
                    self.mm(pg[:, 0:n], w_in[:, c, j * 128:(j + 1) * 128], h[:, c, 0:n], c == 0, c == 7,
                            [("h", s, c), ("w_in", c)], [kg])
                for c in range(8):
                    self.mm(pu[:, 0:n], w_in[:, c, D_FF + j * 128:D_FF + (j + 1) * 128], h[:, c, 0:n], c == 0, c == 7,
                            [("h", s, c), ("w_in", c)], [ku])
                sgj = sg[j % 2]
                self.act(sgj[:, 0:n], pg[:, 0:n], AF.Silu, [kg], [("sg", j % 2)])
                self.tt("dve", aT[:, j, 0:n], sgj[:, 0:n], pu[:, 0:n], ALU.mult, [("sg", j % 2), ku], [("aT", j)])
            for oc in range(8):
                po, ko = self.psum()
                for j in range(22):
                    self.mm(po[:, 0:n], w_out[:, j, oc * 128:(oc + 1) * 128], aT[:, j, 0:n], j == 0, j == 21,
                            [("aT", j), ("w_out", j)], [ko])
                self.stt("dve", xT[:, oc, 0:n], po[:, 0:n], 0.5, xT[:, oc, 0:n], ALU.mult, ALU.add, [ko, kx], [kx])
            self.store_xT(si, t0, n, xT, kx)

        self.tile_loop(tiles, load, compute)
        self.S.barrier()

    def out_phase(self):
        A, T = self.A, self.T
        A.reset()
        dr = self.dr
        gcol = A.alloc([8])
        self.eps_col = A.alloc([1])
        self.memset("pool", self.eps_col, EPS, ["eps"])
        self.load_col(gcol, dr["final_norm"], 8, ("gcol",))
        xTs = [A.alloc([8, T]) for _ in range(2)]
        sq = A.alloc([8, T], BF16)
        rstd = A.alloc([T])
        xo = [A.alloc([T // 128, D]) for _ in range(2)]
        it = 0
        for si, L in enumerate(self.Ls):
            for (t0, n) in seq_tiles(L, T):
                if t0 == 0:
                    continue
                s = it % 2
                it += 1
                xT = xTs[s]
                kx = ("xT", s)
                self.load_xT(False, si, t0, n, xT, kx, None)
                self.rms_rstd(xT, 8, n, rstd, ("rstd",), sq, ("sq",), [kx], D)
                for c in range(8):
                    self.stt("dve", xT[:, c, 0:n], xT[:, c, 0:n], gcol[:, c:c + 1], rstd[:, 0:n], ALU.mult, ALU.mult,
                             [kx, ("rstd",), ("gcol",)], [kx])
                nb = n // 128
                for b in range(nb):
                    for cc in range(2):
                        ps, pk = self.psum()
                        for c4 in range(4):
                            c = cc * 4 + c4
                            self.tr(ps[:, c4 * 128:(c4 + 1) * 128], xT[:, c, b * 128:(b + 1) * 128], self.ident,
                                    [kx, "c_ident"], [pk])
                        self.cp("act" if cc == 0 else "dve", xo[s][:, b, cc * 512:(cc + 1) * 512], ps[:, :], [pk], [("xo", s)])
                dst = dr[f"y{si}"][t0 - N_META:t0 - N_META + n, :].rearrange("(b p) d -> p b d", p=128)
                self.dma("sp", dst, xo[s][:, 0:nb, :], [("xo", s)], [("y", si, t0)])
        self.S.barrier()


    def load_bcast(self, dst, vec, key, q="sp"):
        self.dma(q, dst, vec.partition_broadcast(128), (), [key])

    def norm_h(self, xT, n, gcol, h, s, kx):
        rstd, sq = self.rstd_t, self.sq_t
        self.rms_rstd(xT, 8, n, rstd, ("rstd",), sq, ("sq",), [kx], D)
        for c in range(8):
            self.stt("dve", h[:, c, 0:n], xT[:, c, 0:n], gcol[:, c:c + 1], rstd[:, 0:n], ALU.mult, ALU.mult,
                     [kx, ("rstd",), ("gcol",)], [("h", s, c)])

    def proj1_phase(self, l):
        A, T, dr = self.A, self.T, self.dr
        A.reset()
        NB = T // 128
        wA = A.alloc([8, 2720], BF16)
        wdt = A.alloc([8, 64], BF16)
        wkr_rot = A.alloc([8, 96], BF16)
        wuq = A.alloc([3, 16, 96], BF16)
        wuq_rot = A.alloc([3, 16, 96], BF16)
        wuk = A.alloc([2, 16, 64], BF16)
        wv = A.alloc([2, 16, 64], BF16)
        gcol = A.alloc([8])
        qg = A.alloc([3])
        kvg = A.alloc([2])
        self.eps_col = A.alloc([1])
        dtb = A.alloc([64])
        invc = A.alloc([1])
        pos0 = A.alloc([T])
        self.memset("pool", self.eps_col, EPS, ["eps"])
        self.load_col(gcol, dr["mix_norm"][l], 8, ("gcol",))
        self.load_col(qg, dr["q_norm"][l], 3, ("qg",))
        self.load_col(kvg, dr["kv_norm"][l], 2, ("kvg",))
        self.load_bcast(dtb, dr["dt_bias"][l].rearrange("a h -> (a h)"), ("dtb",))
        w2 = dr["w_in"][l]
        self.load_w_bf16(wA, w2, 8, ("wA",), 0, 2720)
        self.load_w_bf16(wdt, w2, 8, ("wdt",), O_DT, 64)
        self.memset("pool", wkr_rot[:, :, 0:64], 0.0, [("wkr_rot", c) for c in range(8)])
        for c in range(8):
            self.dma("pool", wkr_rot[:, c, 64:80], w2[c * 128:(c + 1) * 128, O_KR + 16:O_KR + 32], (), [("wkr_rot", c)])
            self.dma("pool", wkr_rot[:, c, 80:96], w2[c * 128:(c + 1) * 128, O_KR:O_KR + 16], (), [("wkr_rot", c)])
        self.ts("dve", wkr_rot[:, :, 64:80], wkr_rot[:, :, 64:80], -1.0, None, ALU.mult, None,
                [("wkr_rot", c) for c in range(8)], [("wkr_rot", c) for c in range(8)])
        wq = dr["w_uq"][l]
        for c in range(3):
            src = wq[c * 128:(c + 1) * 128, :].rearrange("p (h d) -> p h d", d=96)
            self.dma("pool", wuq[:, c, :, :], src, (), [("wuq", c)])
            self.memset("pool", wuq_rot[:, c, :, 0:64], 0.0, [("wuq_rot", c)])
            self.dma("pool", wuq_rot[:, c, :, 64:80], src[:, :, 80:96], (), [("wuq_rot", c)])
            self.dma("pool", wuq_rot[:, c, :, 80:96], src[:, :, 64:80], (), [("wuq_rot", c)])
        self.ts("dve", wuq_rot[:, :, :, 64:80], wuq_rot[:, :, :, 64:80], -1.0, None, ALU.mult, None,
                [("wuq_rot", c) for c in range(3)], [("wuq_rot", c) for c in range(3)])
        wkv = dr["w_ukv"][l]
        for c in range(2):
            src = wkv[c * 128:(c + 1) * 128, :].rearrange("p (h d) -> p h d", d=128)
            self.dma("pool", wuk[:, c, :, :], src[:, :, 0:64], (), [("wuk", c)])
            self.dma("pool", wv[:, c, :, :], src[:, :, 64:128], (), [("wv", c)])
        jcol = A.alloc([1])
        jm = A.alloc([1])
        self.S.add("pool", lambda e: e.iota(jcol[64:96, :], pattern=[[0, 1]], base=0, channel_multiplier=1,
                                            allow_small_or_imprecise_dtypes=True), (), ["jcol"])
        self.S.add("pool", lambda e: e.affine_select(out=jm[64:96, :], in_=self.ones_f[64:96, 0:1], pattern=[[0, 1]], compare_op=ALU.is_ge,
                                                      fill=0.0, base=-16, channel_multiplier=1), ["c_onesf"], ["jm"])
        self.stt("dve", jcol[64:96, :], jm[64:96, :], -16.0, jcol[64:96, :], ALU.mult, ALU.add, ["jcol", "jm"], ["jcol"])
        self.act(invc[64:96, :], jcol[64:96, :], AF.Exp, ["jcol"], ["invc"], scale=-float(np.log(10000.0) / 16.0))
        self.S.add("pool", lambda e: e.iota(pos0[64:96, :], pattern=[[1, T]], base=0, channel_multiplier=0,
                                            allow_small_or_imprecise_dtypes=True), (), ["pos0"])

        xTs = [A.alloc([8, T]) for _ in range(2)]
        hs = [A.alloc([8, T], BF16) for _ in range(2)]
        self.sq_t = A.alloc([8, T], BF16)
        self.rstd_t = A.alloc([T])
        cq = A.alloc([3, T])
        cqn = A.alloc([3, T], BF16)
        ckv = A.alloc([2, T])
        ckvn = A.alloc([2, T], BF16)
        rq = A.alloc([T])
        cs = [A.alloc([T]) for _ in range(2)]
        ang = A.alloc([T])
        angi = A.alloc([T]).bitcast(mybir.dt.int32)
        angf = A.alloc([T])
        kr_sb = A.alloc([T])
        krr_sb = A.alloc([T])
        kro = A.alloc([T], BF16)
        qo = [A.alloc([T], BF16) for _ in range(2)]
        qt1 = A.alloc([T])
        qt2 = A.alloc([T])
        ko = [A.alloc([T], BF16) for _ in range(2)]
        vo = [A.alloc([NB, 16, 65], BF16) for _ in range(2)]
        szo = [A.alloc([NB, 2048], BF16) for _ in range(2)]
        dto = [A.alloc([NB, 64]) for _ in range(2)]
        dtt = A.alloc([64])
        dtt2 = A.alloc([64])
        for s in range(2):
            self.memset("pool", vo[s][:, :, :, 64:65], 1.0, [("vo", s)])
        TWO_PI = float(2 * np.pi)
        tiles = [(si, t0, n) for si, L in enumerate(self.Ls) for (t0, n) in seq_tiles(L, T)]

        def load(i, tl):
            si, t0, n = tl
            self.load_xT(False, si, t0, n, xTs[i % 2], ("xT", i % 2), None)

        def trig(dst, n, t0, shift, key):
            self.ts("dve", ang[64:96, 0:n], pos0[64:96, 0:n], float(t0), None, ALU.add, None, ["pos0"], ["ang"])
            self.ts("dve", ang[64:96, 0:n], ang[64:96, 0:n], invc[64:96, :], None, ALU.mult, None, ["ang", "invc"], ["ang"])
            if shift != 0.0:
                self.ts("dve", ang[64:96, 0:n], ang[64:96, 0:n], shift, None, ALU.add, None, ["ang"], ["ang"])
            self.ts("dve", angf[64:96, 0:n], ang[64:96, 0:n], 1.0 / TWO_PI, None, ALU.mult, None, ["ang"], ["angf"])
            self.cp("dve", angi[64:96, 0:n], angf[64:96, 0:n], ["angf"], ["angi"])
            self.cp("dve", angf[64:96, 0:n], angi[64:96, 0:n], ["angi"], ["angf"])
            self.stt("dve", ang[64:96, 0:n], angf[64:96, 0:n], -TWO_PI, ang[64:96, 0:n], ALU.mult, ALU.add, ["angf", "ang"], ["ang"])
            self.ts("dve", ang[64:96, 0:n], ang[64:96, 0:n], float(np.pi), -float(np.pi), ALU.min, ALU.max, ["ang"], ["ang"])
            self.act(dst[64:96, 0:n], ang[64:96, 0:n], AF.Sin, ["ang"], [key])

        def small_norm(src_ps_list, raw, nch, gc, gkey, outn, okey, n, dim):
            for c, (ps_, pk_) in enumerate(src_ps_list):
                self.cp("act", raw[:, c, 0:n], ps_[:, 0:n], [pk_], [(okey, "raw", c)])
            self.rms_rstd(raw, nch, n, rq, ("rq",), self.sq_t, ("sq",), [(okey, "raw", c) for c in range(nch)], dim)
            for c in range(nch):
                self.stt("dve", outn[:, c, 0:n], raw[:, c, 0:n], gc[:, c:c + 1], rq[:, 0:n], ALU.mult, ALU.mult,
                         [(okey, "raw", c), ("rq",), gkey], [(okey, c)])

        def compute(i, tl):
            si, t0, n = tl
            s = i % 2
            g0 = self.offs[si] + t0
            xT, h = xTs[s], hs[s]
            kx = ("xT", s)
            self.norm_h(xT, n, gcol, h, s, kx)
            hk = [("h", s, c) for c in range(8)]

            def fm(wt, wkey, col0, ncol):
                ps_, pk_ = self.psum()
                for c in range(8):
                    self.mm(ps_[0:ncol, 0:n], wt[:, c, col0:col0 + ncol], h[:, c, 0:n], c == 0, c == 7, [("h", s, c), (wkey, c)], [pk_])
                return ps_, pk_

            small_norm([fm(wA, "wA", O_CQ + c * 128, 128) for c in range(3)], cq, 3, qg, ("qg",), cqn, "cqn", n, Q_LORA)
            small_norm([fm(wA, "wA", O_CKV + c * 128, 128) for c in range(2)], ckv, 2, kvg, ("kvg",), ckvn, "ckvn", n, KV_LORA)
            trig(cs[1], n, t0, 0.0, "sin")
            trig(cs[0], n, t0, float(np.pi / 2), "cos")
            pk1, kk1 = fm(wA, "wA", O_KR - 64, 96)
            pk2, kk2 = fm(wkr_rot, "wkr_rot", 0, 96)
            self.tt("dve", kr_sb[64:96, 0:n], pk1[64:96, 0:n], cs[0][64:96, 0:n], ALU.mult, [kk1, "cos"], ["kr_sb"])
            self.tt("dve", krr_sb[64:96, 0:n], pk2[64:96, 0:n], cs[1][64:96, 0:n], ALU.mult, [kk2, "sin"], ["krr_sb"])
            self.tt("dve", kro[64:96, 0:n], kr_sb[64:96, 0:n], krr_sb[64:96, 0:n], ALU.add, ["kr_sb", "krr_sb"], ["kro"])
            self.dma("sp", dr["KR"][:, g0:g0 + n], kro[64:96, 0:n], ["kro"], [("KR", g0)])
            SC = float(96.0 ** -0.5)
            for hd in range(HEADS):
                b = hd % 2
                pq, kq = self.psum()
                for c in range(3):
                    self.mm(pq[0:96, 0:n], wuq[:, c, hd, :], cqn[:, c, 0:n], c == 0, c == 2, [("cqn", c), ("wuq", c)], [kq])
                pr, krk = self.psum()
                for c in range(3):
                    self.mm(pr[0:96, 0:n], wuq_rot[:, c, hd, :], cqn[:, c, 0:n], c == 0, c == 2, [("cqn", c), ("wuq_rot", c)], [krk])
                self.act(qo[b][0:64, 0:n], pq[0:64, 0:n], AF.Copy, [kq], [("qo", b)], scale=SC)
                self.stt("dve", qt1[64:96, 0:n], pq[64:96, 0:n], SC, cs[0][64:96, 0:n], ALU.mult, ALU.mult, [kq, "cos"], ["qt1"])
                self.stt("dve", qt2[64:96, 0:n], pr[64:96, 0:n], SC, cs[1][64:96, 0:n], ALU.mult, ALU.mult, [krk, "sin"], ["qt2"])
                self.tt("dve", qo[b][64:96, 0:n], qt1[64:96, 0:n], qt2[64:96, 0:n], ALU.add, ["qt1", "qt2"], [("qo", b)])
                self.dma("sp", dr["QT"][hd, :, g0:g0 + n], qo[b][0:96, 0:n], [("qo", b)], [("QT", hd, g0)])
                pkn, kkn = self.psum()
                for c in range(2):
                    self.mm(pkn[0:64, 0:n], wuk[:, c, hd, :], ckvn[:, c, 0:n], c == 0, c == 1, [("ckvn", c), ("wuk", c)], [kkn])
                self.cp("act", ko[b][0:64, 0:n], pkn[0:64, 0:n], [kkn], [("ko", b)])
                self.dma("sp", dr["KT"][hd, :, g0:g0 + n], ko[b][0:64, 0:n], [("ko", b)], [("KT", hd, g0)])
            nb = max(1, n // 128)
            pn = min(n, 128)
            for bk in range(nb):
                tsl = slice(bk * 128, bk * 128 + pn)
                for half in range(2):
                    pv, kv = self.psum()
                    for c in range(2):
                        self.mm(pv[0:pn, :], ckvn[:, c, tsl], wv[:, c, half * 8:(half + 1) * 8, :], c == 0, c == 1,
                                [("ckvn", c), ("wv", c)], [kv])
                    self.cp("act" if half == 0 else "dve", vo[s][0:pn, bk, half * 8:(half + 1) * 8, 0:64],
                            pv[0:pn, :].rearrange("p (h d) -> p h d", d=64), [kv], [("vo", s)])
                for q4 in range(4):
                    pz, kz = self.psum()
                    for c in range(8):
                        self.mm(pz[0:pn, :], h[:, c, tsl], wA[:, c, O_Z + q4 * 512:O_Z + (q4 + 1) * 512], c == 0, c == 7,
                                [("h", s, c), ("wA", c)], [kz])
                    self.act(szo[s][0:pn, bk, q4 * 512:(q4 + 1) * 512], pz[0:pn, :], AF.Silu, [kz], [("szo", s)])
                pd, kd = self.psum()
                for c in range(8):
                    self.mm(pd[0:pn, 0:64], h[:, c, tsl], wdt[:, c, :], c == 0, c == 7, [("h", s, c), ("wdt", c)], [kd])
                self.tt("dve", dtt[0:pn, :], pd[0:pn, 0:64], dtb[0:pn, :], ALU.add, [kd, ("dtb",)], ["dtt"])
                self.act(dtt2[0:pn, :], dtt[0:pn, :], AF.Abs, ["dtt"], ["dtt2"])
                self.act(dtt2[0:pn, :], dtt2[0:pn, :], AF.Exp, ["dtt2"], ["dtt2"], scale=-1.0)
                self.act(dtt2[0:pn, :], dtt2[0:pn, :], AF.Ln, ["dtt2"], ["dtt2"], bias=1.0)
                self.stt("dve", dto[s][0:pn, bk, :], dtt[0:pn, :], 0.0, dtt2[0:pn, :], ALU.max, ALU.add, ["dtt", "dtt2"], [("dto", s)])
            if n >= 128:
                def tokv(ap3):
                    return ap3.rearrange("(b p) f -> p b f", p=128)
                self.dma("sp", tokv(dr["V"][g0:g0 + n, :]), vo[s][:, 0:nb, :, :].rearrange("p b h d -> p b (h d)"), [("vo", s)], [("V", g0)])
                self.dma("sp", tokv(dr["SZ"][g0:g0 + n, :]), szo[s][:, 0:nb, :], [("szo", s)], [("SZ", g0)])
                self.dma("sp", tokv(dr["DT"][g0:g0 + n, :]), dto[s][:, 0:nb, :], [("dto", s)], [("DT", g0)])
            else:
                self.dma("sp", dr["V"][g0:g0 + n, :], vo[s][0:n, 0, :, :].rearrange("p h d -> p (h d)"), [("vo", s)], [("V", g0)])
                self.dma("sp", dr["SZ"][g0:g0 + n, :], szo[s][0:n, 0, :], [("szo", s)], [("SZ", g0)])
                self.dma("sp", dr["DT"][g0:g0 + n, :], dto[s][0:n, 0, :], [("dto", s)], [("DT", g0)])

        self.tile_loop(tiles, load, compute)
        self.S.barrier()

    def proj2_phase(self, l):
        A, T, dr = self.A, self.T, self.dr
        A.reset()
        wB = A.alloc([8, 5120], BF16)
        gcol = A.alloc([8])
        self.eps_col = A.alloc([1])
        self.memset("pool", self.eps_col, EPS, ["eps"])
        self.load_col(gcol, dr["mix_norm"][l], 8, ("gcol",))
        w2 = dr["w_in"][l]
        for c in range(8):
            self.dma("pool", wB[:, c, 0:3072], w2[c * 128:(c + 1) * 128, O_XBC:O_XBC + 3072], (), [("wB", c)], max_dma_last_dim=4096)
            self.dma("pool", wB[:, c, 3072:5120], w2[c * 128:(c + 1) * 128, O_GATE:O_GATE + 2048], (), [("wB", c)], max_dma_last_dim=4096)
        xTs = [A.alloc([8, T]) for _ in range(2)]
        hs = [A.alloc([8, T], BF16) for _ in range(2)]
        self.sq_t = A.alloc([8, T], BF16)
        self.rstd_t = A.alloc([T])
        xo = [A.alloc([24, T], BF16) for _ in range(2)]
        go = [A.alloc([16, T], BF16) for _ in range(2)]
        tiles = [(si, t0, n) for si, L in enumerate(self.Ls) for (t0, n) in seq_tiles(L, T)]

        def load(i, tl):
            si, t0, n = tl
            self.load_xT(False, si, t0, n, xTs[i % 2], ("xT", i % 2), None)

        def compute(i, tl):
            si, t0, n = tl
            s = i % 2
            g0 = self.offs[si] + t0
            xT, h = xTs[s], hs[s]
            self.norm_h(xT, n, gcol, h, s, ("xT", s))
            for oc in range(40):
                ps_, pk_ = self.psum()
                for c in range(8):
                    self.mm(ps_[:, 0:n], wB[:, c, oc * 128:(oc + 1) * 128], h[:, c, 0:n], c == 0, c == 7, [("h", s, c), ("wB", c)], [pk_])
                if oc < 24:
                    self.cp("dve" if oc % 2 else "act", xo[s][:, oc, 0:n], ps_[:, 0:n], [pk_], [("xo", s)])
                else:
                    self.act(go[s][:, oc - 24, 0:n], ps_[:, 0:n], AF.Sigmoid, [pk_], [("go", s)])
            self.dma("sp", dr["XBC"][:, :, g0:g0 + n].rearrange("c p t -> p c t"), xo[s][:, :, 0:n], [("xo", s)], [("XBC", g0)])
            self.dma("sp", dr["GT"][:, :, g0:g0 + n].rearrange("c p t -> p c t"), go[s][:, :, 0:n], [("go", s)], [("GT", g0)])

        self.tile_loop(tiles, load, compute)
        self.S.barrier()

    def attn_phase(self, l):
        A, dr = self.A, self.dr
        A.reset()
        QT_ = 512
        Lmax = max(self.Ls)
        nkb_max = 1 + (Lmax - N_META) // 128
        Kt = [A.alloc([Lmax], BF16) for _ in range(2)]
        Vt = [A.alloc([nkb_max, 65], BF16) for _ in range(2)]
        Qt = [A.alloc([QT_], BF16) for _ in range(2)]
        Pt = [A.alloc([QT_], BF16) for _ in range(4)]
        den = A.alloc([QT_])
        rb = A.alloc([QT_])
        oT = [A.alloc([QT_], BF16) for _ in range(2)]
        pi = 0
        hi = 0
        for si, L in enumerate(self.Ls):
            off = self.offs[si]
            kblocks = [(0, N_META)] + [(N_META + 128 * i, 128) for i in range((L - N_META) // 128)]
            qtiles = seq_tiles(L, QT_)
            for hd in range(HEADS):
                hs_ = hi % 2
                hi += 1
                K, V = Kt[hs_], Vt[hs_]
                kK, kV = ("K", hs_), ("V", hs_)
                self.dma("sp", K[64:96, 0:L], dr["KR"][:, off:off + L], [("KR",)], [kK])
                self.dma("sp", K[0:64, 0:L], dr["KT"][hd, :, off:off + L], [("KT",)], [kK])
                self.dma("sp", V[0:N_META, 0, :], dr["V"][off:off + N_META, hd * 65:(hd + 1) * 65], [("V",)], [kV])
                nfull = (L - N_META) // 128
                self.dma("sp", V[:, 1:1 + nfull, :],
                         dr["V"][off + N_META:off + L, hd * 65:(hd + 1) * 65].rearrange("(b p) d -> p b d", p=128), [("V",)], [kV])
                for qi, (q0, qn) in enumerate(qtiles):
                    qs = (hi * 100 + qi) % 2
                    Q = Qt[qs]
                    kQ = ("Q", qs)
                    self.dma("sp", Q[0:96, 0:qn], dr["QT"][hd, :, off + q0:off + q0 + qn], [("QT",)], [kQ])
                    po, ko_ = self.ps[6 + qs], ("ps", 6 + qs)
                    for kb, (k0, kn) in enumerate(kblocks):
                        psc, ksc = self.ps[pi % 6], ("ps", pi % 6)
                        P, kP = Pt[pi % 4], ("P", pi % 4)
                        pi += 1
                        self.mm(psc[0:kn, 0:qn], K[0:96, k0:k0 + kn], Q[0:96, 0:qn], True, True, [kK, kQ], [ksc])
                        self.act(P[0:kn, 0:qn], psc[0:kn, 0:qn], AF.Exp, [ksc], [kP])
                        self.mm(po[0:65, 0:qn], V[0:kn, kb, :], P[0:kn, 0:qn], kb == 0, kb == len(kblocks) - 1, [kV, kP], [ko_])
                    self.cp("dve", den[64:65, 0:qn], po[64:65, 0:qn], [ko_], ["den"])
                    self.S.add("dve", lambda e, qn=qn: e.reciprocal(out=den[64:65, 0:qn], in_=den[64:65, 0:qn]), ["den"], ["den"])
                    pb, kb_ = self.ps[pi % 6], ("ps", pi % 6)
                    pi += 1
                    self.mm(pb[0:64, 0:qn], self.ones_f[64:65, 0:64], den[64:65, 0:qn], True, True, ["den", "c_onesf"], [kb_])
                    self.cp("act", rb[0:64, 0:qn], pb[0:64, 0:qn], [kb_], ["rb"])
                    self.tt("dve", oT[qs][0:64, 0:qn], po[0:64, 0:qn], rb[0:64, 0:qn], ALU.mult, [ko_, "rb"], [("oT", qs)])
                    self.dma("sp", dr["OA"][hd // 2, (hd % 2) * 64:(hd % 2) * 64 + 64, off + q0:off + q0 + qn], oT[qs][0:64, 0:qn],
                             [("oT", qs)], [("OA", hd, off + q0)])
        self.S.barrier()

    def merge_phase(self, l):
        A, T, dr = self.A, self.T, self.dr
        A.reset()
        wb = A.alloc([24, D], BF16)
        wo = A.alloc([8, D], BF16)
        self.load_w_bf16(wb, dr["w_branch"][l], 24, ("wb",))
        self.load_w_bf16(wo, dr["w_out"][l], 8, ("wo",))
        xTs = [A.alloc([8, T]) for _ in range(2)]
        oas = [A.alloc([8, T], BF16) for _ in range(2)]
        oms = [A.alloc([16, T], BF16) for _ in range(2)]
        gts = [A.alloc([16, T], BF16) for _ in range(2)]
        mix = A.alloc([8, T], BF16)
        ta = [A.alloc([T]) for _ in range(2)]
        tm = [A.alloc([T]) for _ in range(2)]
        tiles = [(si, t0, n) for si, L in enumerate(self.Ls) for (t0, n) in seq_tiles(L, T)]

        def load(i, tl):
            si, t0, n = tl
            s = i % 2
            g0 = self.offs[si] + t0
            self.load_xT(False, si, t0, n, xTs[s], ("xT", s), None)
            self.dma("sp", oas[s][:, :, 0:n], dr["OA"][:, :, g0:g0 + n].rearrange("c p t -> p c t"), [("OA",)], [("oa", s)])
            self.dma("sp", oms[s][:, :, 0:n], dr["OM"][:, :, g0:g0 + n].rearrange("c p t -> p c t"), [("OM",)], [("om", s)])
            self.dma("sp", gts[s][:, :, 0:n], dr["GT"][:, :, g0:g0 + n].rearrange("c p t -> p c t"), [("GT",)], [("gt", s)])

        def compute(i, tl):
            si, t0, n = tl
            s = i % 2
            xT = xTs[s]
            kx = ("xT", s)
            for oc in range(8):
                pa, ka = self.psum()
                for c in range(8):
                    self.mm(pa[:, 0:n], wb[:, c, oc * 128:(oc + 1) * 128], oas[s][:, c, 0:n], c == 0, c == 7, [("oa", s), ("wb", c)], [ka])
                pm, km = self.psum()
                for c in range(16):
                    self.mm(pm[:, 0:n], wb[:, 8 + c, oc * 128:(oc + 1) * 128], oms[s][:, c, 0:n], c == 0, c == 15, [("om", s), ("wb", 8 + c)], [km])
                b = oc % 2
                self.tt("dve", ta[b][:, 0:n], pa[:, 0:n], gts[s][:, oc, 0:n], ALU.mult, [ka, ("gt", s)], [("ta", b)])
                self.tt("dve", tm[b][:, 0:n], pm[:, 0:n], gts[s][:, 8 + oc, 0:n], ALU.mult, [km, ("gt", s)], [("tm", b)])
                self.tt("pool", mix[:, oc, 0:n], ta[b][:, 0:n], tm[b][:, 0:n], ALU.add, [("ta", b), ("tm", b)], [("mix", oc)])
            for oc in range(8):
                po, ko_ = self.psum()
                for c in range(8):
                    self.mm(po[:, 0:n], wo[:, c, oc * 128:(oc + 1) * 128], mix[:, c, 0:n], c == 0, c == 7, [("mix", c), ("wo", c)], [ko_])
                self.tt("dve", xT[:, oc, 0:n], po[:, 0:n], xT[:, oc, 0:n], ALU.add, [ko_, kx], [kx])
            self.store_xT(si, t0, n, xT, kx)

        self.tile_loop(tiles, load, compute)
        self.S.barrier()


    def ssd_phase(self, l):
        self.ssd_conv(l)
        self.ssd_scan(l, 1)
        self.ssd_scan(l, 0)

    def ssd_conv(self, l):
        A, T, dr = self.A, self.T, self.dr
        A.reset()
        NB = T // 128
        cw_rows = A.alloc([128])
        cb_rows = A.alloc([128])
        wcol = A.alloc([120])
        bcol = A.alloc([24])
        brow_f = A.alloc([3072])
        brow = A.alloc([3072], BF16)
        dg = A.alloc([24, 5, 128], BF16)
        self.dma("sp", cw_rows[0:120, :], dr["conv_w"][l].rearrange("j (c p) -> (j c) p", p=128), (), ["cw_rows"])
        self.dma("sp", cb_rows[0:24, :], dr["conv_b"][l].rearrange("(c p) -> c p", p=128), (), ["cb_rows"])
        self.dma("sp", brow_f[0:1, :], dr["conv_b"][l].rearrange("(o n) -> o n", o=1), (), ["brow_f"])
        self.cp("dve", brow[0:1, :], brow_f[0:1, :], ["brow_f"], ["brow"])
        ps_, pk_ = self.psum()
        self.tr(ps_[:, 0:120], cw_rows[0:120, :], self.ident[0:120, 0:120], ["cw_rows", "c_ident"], [pk_])
        self.cp("dve", wcol, ps_[:, 0:120], [pk_], ["wcol"])
        ps_, pk_ = self.psum()
        self.tr(ps_[:, 0:24], cb_rows[0:24, :], self.ident[0:24, 0:24], ["cb_rows", "c_ident"], [pk_])
        self.cp("dve", bcol, ps_[:, 0:24], [pk_], ["bcol"])
        for c in range(24):
            for j in range(5):
                self.ts("dve" if (c + j) % 2 else "pool", dg[:, c, j, :], self.identb, wcol[:, j * 24 + c:j * 24 + c + 1], None,
                        ALU.mult, None, ["c_identb", "wcol"], [("dg", c)])
        xw = [A.alloc([24, T + 4], BF16) for _ in range(2)]
        bco = [A.alloc([8, T], BF16) for _ in range(2)]
        xso = [A.alloc([NB, 2048], BF16) for _ in range(2)]
        bto = [A.alloc([NB, 512], BF16) for _ in range(2)]
        tiles = [(si, t0, n, L) for si, L in enumerate(self.Ls) for (t0, n) in seq_tiles(L, T)]

        def load(i, tl):
            si, t0, n, L = tl
            s = i % 2
            off = self.offs[si]
            lo, hi = max(t0 - 2, 0), min(t0 + n + 2, L)
            k = ("xw", s)
            if t0 - 2 < 0:
                self.memset("pool", xw[s][:, :, 0:2], 0.0, [k])
            if t0 + n + 2 > L:
                self.memset("pool", xw[s][:, :, n + 2:n + 4], 0.0, [k])
            self.dma("sp", xw[s][:, :, lo - (t0 - 2):hi - (t0 - 2)], dr["XBC"][:, :, off + lo:off + hi].rearrange("c p t -> p c t"), (), [k])

        def compute(i, tl):
            si, t0, n, L = tl
            s = i % 2
            g0 = self.offs[si] + t0
            k = ("xw", s)
            for cc in range(8):
                cidx = 16 + cc
                ps_, pk_ = self.psum()
                for j in range(5):
                    self.mm(ps_[:, 0:n], dg[:, cidx, j, :], xw[s][:, cidx, j:j + n], j == 0, j == 4, [k, ("dg", cidx)], [pk_])
                self.act(bco[s][:, cc, 0:n], ps_[:, 0:n], AF.Silu, [pk_, "bcol"], [("bco", s)], bias=bcol[:, cidx:cidx + 1])
            self.dma("sp", dr["BT"][:, :, g0:g0 + n].rearrange("c p t -> p c t"), bco[s][:, 0:4, 0:n], [("bco", s)], [("BT", g0)])
            self.dma("sp", dr["CT"][:, :, g0:g0 + n].rearrange("c p t -> p c t"), bco[s][:, 4:8, 0:n], [("bco", s)], [("CT", g0)])
            nb = max(1, n // 128)
            pn = min(n, 128)
            for bk in range(nb):
                for grp in range(5):
                    ps_, pk_ = self.psum()
                    for cc in range(4):
                        cidx = grp * 4 + cc
                        osl = ps_[0:pn, cc * 128:(cc + 1) * 128]
                        for j in range(5):
                            self.mm(osl, xw[s][:, cidx, j + bk * 128:j + bk * 128 + pn], dg[:, cidx, j, :], j == 0, False,
                                    [k, ("dg", cidx)], [pk_])
                        self.mm(osl, self.onesb[0:1, 0:pn], brow[0:1, cidx * 128:(cidx + 1) * 128], False, True, ["c_onesb", "brow"], [pk_])
                    if grp < 4:
                        self.act(xso[s][0:pn, bk, grp * 512:(grp + 1) * 512], ps_[0:pn, :], AF.Silu, [pk_], [("xso", s)])
                    else:
                        self.act(bto[s][0:pn, bk, :], ps_[0:pn, :], AF.Silu, [pk_], [("bto", s)])
            if n >= 128:
                self.dma("sp", dr["XS"][g0:g0 + n, :].rearrange("(b p) f -> p b f", p=128), xso[s][:, 0:nb, :], [("xso", s)], [("XS", g0)])
                self.dma("sp", dr["BTOK"][g0:g0 + n, :].rearrange("(b p) f -> p b f", p=128), bto[s][:, 0:nb, :], [("bto", s)], [("BTOK", g0)])
            else:
                self.dma("sp", dr["XS"][g0:g0 + n, :], xso[s][0:n, 0, :], [("xso", s)], [("XS", g0)])
                self.dma("sp", dr["BTOK"][g0:g0 + n, :], bto[s][0:n, 0, :], [("bto", s)], [("BTOK", g0)])

        self.tile_loop(tiles, load, compute)
        self.S.barrier()

    def ssd_scan(self, l, d):
        A, dr = self.A, self.dr
        A.reset()
        fwd = (d == 0)
        NEGV = -30000.0
        zeros_f = A.alloc([128])
        U = A.alloc([128], BF16)
        Uf = A.alloc([128])
        NEGf = A.alloc([128])
        NEG4 = A.alloc([4, 128], BF16)
        a_b = A.alloc([32])
        dsk = A.alloc([32])
        ssmn = A.alloc([2048])
        self.eps_col = A.alloc([1])
        self.memset("pool", self.eps_col, EPS, ["eps"])
        self.memset("pool", zeros_f, 0.0, ["zeros_f"])
        pat, cm = ([[1, 128]], -1) if fwd else ([[-1, 128]], 1)
        self.S.add("pool", lambda e: e.affine_select(out=Uf, in_=self.ones_f, pattern=pat, compare_op=ALU.is_ge, fill=0.0, base=0,
                                                      channel_multiplier=cm), ["c_onesf"], ["Uf"])
        self.S.add("pool", lambda e: e.affine_select(out=NEGf, in_=zeros_f, pattern=pat, compare_op=ALU.is_ge, fill=NEGV, base=0,
                                                      channel_multiplier=cm), ["zeros_f"], ["NEGf"])
        self.cp("dve", U, Uf, ["Uf"], ["U"])
        for r in range(4):
            self.cp("dve", NEG4[:, r, :], NEGf, ["NEGf"], ["NEG4"])
        self.load_bcast(a_b, dr["a_log"][l, d], "a_b")
        self.act(a_b, a_b, AF.Exp, ["a_b"], ["a_b"])
        self.ts("dve", a_b, a_b, -1.0, None, ALU.mult, None, ["a_b"], ["a_b"])
        if fwd:
            self.load_bcast(dsk, dr["d_skip"][l], "dsk")
            self.load_bcast(ssmn, dr["ssm_norm"][l], "ssmn")
        xs = [A.alloc([32, 64], BF16) for _ in range(2)]
        btok = [A.alloc([512], BF16) for _ in range(2)]
        BT = [A.alloc([4, 128], BF16) for _ in range(2)]
        CT = [A.alloc([4, 128], BF16) for _ in range(2)]
        dtt = [A.alloc([64]) for _ in range(2)]
        if fwd:
            ybt = [A.alloc([2048]) for _ in range(2)]
            szt = [A.alloc([2048], BF16) for _ in range(2)]
            om = A.alloc([2048])
            omT = [A.alloc([16, 128], BF16) for _ in range(2)]
            ss = A.alloc([4])
            sqj = A.alloc([512])
            t2 = A.alloc([32, 64])
        da_bf = A.alloc([32], BF16)
        W = A.alloc([32, 128], BF16)
        FT = A.alloc([64])
        negF = A.alloc([32])
        dF = A.alloc([32])
        expF = A.alloc([32])
        decs = A.alloc([32])
        cdec = A.alloc([32])
        dtd = A.alloc([32])
        xr = A.alloc([32, 64], BF16)
        xdec = A.alloc([32, 64], BF16)
        run = A.alloc([32, 64])
        prev = A.alloc([2048], BF16)
        CBs = A.alloc([4, 128])
        LT = [A.alloc([8, 128]) for _ in range(2)]
        MT = [A.alloc([8, 128], BF16) for _ in range(2)]
        t1 = [A.alloc([8, 64]) for _ in range(2)]
        yt = [A.alloc([2048]) for _ in range(2)]

        chunks = []
        for si, L in enumerate(self.Ls):
            cl = [(si, 0, N_META)] + [(si, N_META + 128 * i, 128) for i in range((L - N_META) // 128)]
            if not fwd:
                cl = cl[::-1]
            cl = [c + (j == 0,) for j, c in enumerate(cl)]
            chunks.extend(cl)

        def load(i, ch):
            si, t0, kn, firstc = ch
            s = i % 2
            g0 = self.offs[si] + t0
            self.dma("sp", xs[s][0:kn, :, :].rearrange("p h d -> p (h d)"), dr["XS"][g0:g0 + kn, :], (), [("xs", s)])
            self.dma("sp", btok[s][0:kn, :], dr["BTOK"][g0:g0 + kn, :], (), [("btok", s)])
            self.dma("sp", BT[s][:, :, 0:kn], dr["BT"][:, :, g0:g0 + kn].rearrange("c p t -> p c t"), (), [("BT", s)])
            self.dma("sp", CT[s][:, :, 0:kn], dr["CT"][:, :, g0:g0 + kn].rearrange("c p t -> p c t"), (), [("CT", s)])
            self.dma("sp", dtt[s][0:kn, :], dr["DT"][g0:g0 + kn, :], (), [("dt", s)])
            if fwd:
                self.dma("sp", ybt[s][0:kn, :], dr["YB"][g0:g0 + kn, :], (), [("yb", s)])
                self.dma("sp", szt[s][0:kn, :], dr["SZ"][g0:g0 + kn, :], (), [("sz", s)])

        def compute(i, ch):
            si, t0, kn, firstc = ch
            s = i % 2
            g0 = self.offs[si] + t0
            if firstc:
                self.memset("pool", run, 0.0, [("run", g) for g in range(4)])
            dt_d = dtt[s][0:kn, d * 32:(d + 1) * 32]
            kdt = ("dt", s)
            self.tt("dve", da_bf[0:kn, :], dt_d, a_b[0:kn, :], ALU.mult, [kdt, "a_b"], ["da_bf"])
            self.tt("pool", W[0:kn, :, 0:kn], da_bf[0:kn, :].unsqueeze(2).to_broadcast([kn, 32, kn]),
                    U[0:kn, 0:kn].unsqueeze(1).to_broadcast([kn, 32, kn]), ALU.mult, ["da_bf", "U"], ["W"])
            psF, kF = self.psum()
            self.mm(psF[0:kn, 0:32], U[0:kn, 0:kn], da_bf[0:kn, :], True, True, ["U", "da_bf"], [kF])
            self.mm(psF[:, 32:64], self.onesb[0:kn, :], da_bf[0:kn, :], True, True, ["c_onesb", "da_bf"], [kF])
            self.cp("dve", FT[0:kn, 0:32], psF[0:kn, 0:32], [kF], ["FTa"])
            self.cp("dve", FT[:, 32:64], psF[:, 32:64], [kF], ["FTb"])
            self.ts("dve", negF[0:kn, :], FT[0:kn, 0:32], -1.0, None, ALU.mult, None, ["FTa"], ["negF"])
            self.tt("dve", dF[0:kn, :], FT[0:kn, 32:64], FT[0:kn, 0:32], ALU.subtract, ["FTa", "FTb"], ["dF"])
            self.act(expF[0:kn, :], FT[0:kn, 0:32], AF.Exp, ["FTa"], ["expF"])
            self.act(decs[0:kn, :], dF[0:kn, :], AF.Exp, ["dF"], ["decs"])
            self.act(cdec, FT[:, 32:64], AF.Exp, ["FTb"], ["cdec"])
            self.tt("dve", dtd[0:kn, :], dt_d, decs[0:kn, :], ALU.mult, [kdt, "decs"], ["dtd"])
            kxs = ("xs", s)
            self.tt("pool", xr[0:kn, :, :], xs[s][0:kn, :, :], dt_d.unsqueeze(2).to_broadcast([kn, 32, 64]), ALU.mult, [kxs, kdt], ["xr"])
            self.tt("pool", xdec[0:kn, :, :], xs[s][0:kn, :, :], dtd[0:kn, :].unsqueeze(2).to_broadcast([kn, 32, 64]), ALU.mult,
                    [kxs, "dtd"], ["xdec"])
            self.cp("act", prev, run.rearrange("p h d -> p (h d)"), [("run", g) for g in range(4)], ["prev"])
            psCB, kCB = self.psum()
            for g in range(4):
                self.mm(psCB[0:kn, g * 128:g * 128 + kn], BT[s][:, g, 0:kn], CT[s][:, g, 0:kn], True, True, [("BT", s), ("CT", s)], [kCB])
            self.cp("act", CBs[0:kn, :, :], psCB[0:kn, :].rearrange("p (g l) -> p g l", g=4), [kCB], ["CBs"])
            y = yt[s]
            for g in range(4):
                b = g % 2
                for half in range(2):
                    ps_, pk_ = self.psum()
                    h0 = g * 8 + half * 4
                    self.mm(ps_[0:kn, 0:4 * kn], self.onesb[0:kn, 0:kn], W[0:kn, h0:h0 + 4, 0:kn], True, False, ["c_onesb", "W"], [pk_])
                    self.mm(ps_[0:kn, 0:4 * kn], self.identb[0:kn, 0:kn], NEG4[0:kn, :, 0:kn], False, True, ["c_identb", "NEG4"], [pk_])
                    for hh in range(4):
                        self.act(LT[b][0:kn, half * 4 + hh, 0:kn], ps_[0:kn, hh * kn:(hh + 1) * kn], AF.Exp, [pk_, "negF"], [("LT", b)],
                                 bias=negF[0:kn, h0 + hh:h0 + hh + 1])
                self.tt("dve", MT[b][0:kn, :, 0:kn], LT[b][0:kn, :, 0:kn], CBs[0:kn, g:g + 1, 0:kn].to_broadcast([kn, 8, kn]), ALU.mult,
                        [("LT", b), "CBs"], [("MT", b)])
                psY, kY = self.psum()
                for hh in range(8):
                    self.mm(psY[0:kn, hh * 64:(hh + 1) * 64], MT[b][0:kn, hh, 0:kn], xr[0:kn, g * 8 + hh, :], True, True, [("MT", b), "xr"], [kY])
                psO, kO = self.psum()
                self.mm(psO[0:kn, :], CT[s][:, g, 0:kn], prev[:, g * 512:(g + 1) * 512], True, True, [("CT", s), "prev"], [kO])
                self.tt("dve", t1[b][0:kn, :, :], psO[0:kn, :].rearrange("p (h d) -> p h d", d=64),
                        expF[0:kn, g * 8:(g + 1) * 8].unsqueeze(2).to_broadcast([kn, 8, 64]), ALU.mult, [kO, "expF"], [("t1", b)])
                self.tt("dve", y[0:kn, g * 512:(g + 1) * 512], psY[0:kn, :], t1[b][0:kn, :, :].rearrange("p h d -> p (h d)"), ALU.add,
                        [kY, ("t1", b)], [("y", s)])
                psS, kS = self.psum()
                self.mm(psS[:, :], btok[s][0:kn, g * 128:(g + 1) * 128], xdec[0:kn, g * 8:(g + 1) * 8, :], True, True, [("btok", s), "xdec"], [kS])
                rg = run[:, g * 8:(g + 1) * 8, :]
                self.tt("pool", rg, rg, cdec[:, g * 8:(g + 1) * 8].unsqueeze(2).to_broadcast([128, 8, 64]), ALU.mult,
                        [("run", g), "cdec", "prev"], [("run", g)])
                self.tt("dve", rg, rg, psS[:, :].rearrange("p (h d) -> p h d", d=64), ALU.add, [("run", g), kS], [("run", g)])
            if not fwd:
                self.dma("sp", dr["YB"][g0:g0 + kn, :], y[0:kn, :], [("y", s)], [("YB", g0)])
                return
            ky = ("y", s)
            self.tt("dve", y[0:kn, :], y[0:kn, :], ybt[s][0:kn, :], ALU.add, [ky, ("yb", s)], [ky])
            self.tt("pool", t2[0:kn, :, :], xs[s][0:kn, :, :], dsk[0:kn, :].unsqueeze(2).to_broadcast([kn, 32, 64]), ALU.mult, [kxs, "dsk"], ["t2"])
            self.tt("pool", y[0:kn, :], y[0:kn, :], t2[0:kn, :, :].rearrange("p h d -> p (h d)"), ALU.add, [ky, "t2"], [ky])
            self.tt("dve", y[0:kn, :], y[0:kn, :], szt[s][0:kn, :], ALU.mult, [ky, ("sz", s)], [ky])
            self.memset("pool", ss[0:kn, :], 0.0, ["ss"])
            for g in range(4):
                self.act(sqj[0:kn, :], y[0:kn, g * 512:(g + 1) * 512], AF.Square, [ky, "ss"], ["sqj", "ss"], accum_out=ss[0:kn, g:g + 1])
            self.act(ss[0:kn, :], ss[0:kn, :], AF.Sqrt, ["ss", "eps"], ["ss"], scale=1.0 / 512, bias=self.eps_col[0:kn, :])
            self.S.add("dve", lambda e: e.reciprocal(out=ss[0:kn, :], in_=ss[0:kn, :]), ["ss"], ["ss"])
            self.tt("dve", y[0:kn, :].rearrange("p (g f) -> p g f", g=4), y[0:kn, :].rearrange("p (g f) -> p g f", g=4),
                    ss[0:kn, :].unsqueeze(2).to_broadcast([kn, 4, 512]), ALU.mult, [ky, "ss"], [ky])
            self.tt("pool", om[0:kn, :], y[0:kn, :], ssmn[0:kn, :], ALU.mult, [ky, "ssmn"], ["om"])
            for q4 in range(4):
                ps_, pk_ = self.psum()
                for cc in range(4):
                    c = q4 * 4 + cc
                    self.tr(ps_[:, cc * 128:cc * 128 + kn], om[0:kn, c * 128:(c + 1) * 128], self.ident[0:kn, 0:kn], ["om", "c_ident"], [pk_])
                self.cp("act" if q4 % 2 else "dve", omT[s][:, q4 * 4:(q4 + 1) * 4, 0:kn],
                        ps_[:, :].rearrange("p (c t) -> p c t", c=4)[:, :, 0:kn], [pk_], [("omT", s)])
            self.dma("sp", dr["OM"][:, :, g0:g0 + kn].rearrange("c p t -> p c t"), omT[s][:, :, 0:kn], [("omT", s)], [("OM", g0)])

        self.tile_loop(chunks, load, compute)
        self.S.barrier()


_CACHE = {}


def _get_prog(Ls):
    key = tuple(Ls)
    if key not in _CACHE:
        p = Prog(Ls)
        _CACHE[key] = (p, p.build())
    return _CACHE[key]


WEIGHT_NAMES = ["meta_tokens", "ffn1_norm", "ffn1_w_in", "ffn1_w_out", "mix_norm", "w_in", "q_norm", "w_uq", "kv_norm",
                "w_ukv", "conv_w", "conv_b", "a_log", "dt_bias", "d_skip", "ssm_norm", "w_branch", "w_out", "ffn2_norm",
                "ffn2_w_in", "ffn2_w_out", "final_norm"]


def kernel(**inputs):
    xp = np.asarray(inputs["x_prompt"], dtype=np.float32)
    xs = np.asarray(inputs["x_sample"], dtype=np.float32)
    Ls = (xp.shape[1] + N_META, xs.shape[1] + N_META)
    prog, nc = _get_prog(Ls)
    w = {k: np.ascontiguousarray(np.asarray(inputs[k], dtype=np.float32)) for k in WEIGHT_NAMES}
    in_maps = []
    for c in range(8):
        m = dict(w)
        m["x0"] = np.ascontiguousarray(xp[c])
        m["x1"] = np.ascontiguousarray(xs[c % 4])
        in_maps.append(m)
    res = run_bass_kernel_spmd(nc, in_maps, core_ids=list(range(8)))
    y_prompt = np.stack([res.results[c]["y0"] for c in range(8)], axis=0)
    y_sample = np.stack([res.results[c]["y1"] for c in range(4)], axis=0)
    return (y_prompt, y_sample)
```

```python
import contextlib
import numpy as np
import concourse.bass as bass
import concourse.mybir as mybir
from concourse.bass_utils import run_bass_kernel_spmd

F32 = mybir.dt.float32
BF16 = mybir.dt.bfloat16
AF = mybir.ActivationFunctionType
ALU = mybir.AluOpType
AX = mybir.AxisListType

D = 1024
DEPTH = 2
N_META = 16
HEADS = 16
QK_NOPE = 64
QK_ROPE = 32
V_HEAD = 64
Q_LORA = 384
KV_LORA = 256
D_INNER = 2048
SSM_HEADS = 32
SSM_HD = 64
SSM_GROUPS = 4
D_STATE = 128
D_CONV = 5
CONV_CH = 3072
D_FF = 2816
IN_W = 7904
EPS = 1e-6
O_CQ = 0
O_CKV = 384
O_KR = 640
O_Z = 672
O_XBC = 2720
O_DT = 5792
O_GATE = 5856

ENGS = ("pe", "act", "dve", "pool", "sp")


class Op:
    __slots__ = ("eng", "fn", "dma", "deps", "idx", "signal", "sigval", "dsem", "dval", "dprev")

    def __init__(self, eng, fn, dma):
        self.eng = eng
        self.fn = fn
        self.dma = dma
        self.deps = []
        self.signal = False
        self.sigval = 0
        self.dsem = None
        self.dval = 0
        self.dprev = 0


class Sched:
    def __init__(self, nc, n_dma_sems=(("sp", 44), ("pool", 24), ("act", 16))):
        self.nc = nc
        self.ops = {e: [] for e in ENGS}
        self.lastw = {}
        self.readers = {}
        self.n_dma_sems = dict(n_dma_sems)
        self._bar_mark = {e: 0 for e in ENGS}

    def add(self, eng, fn, reads=(), writes=(), dma=False):
        op = Op(eng, fn, dma)
        deps = set()
        for r in reads:
            w = self.lastw.get(r)
            if w is not None:
                deps.add(w)
        for r in writes:
            w = self.lastw.get(r)
            if w is not None:
                deps.add(w)
            for rd in self.readers.get(r, ()):
                deps.add(rd)
        for r in reads:
            self.readers.setdefault(r, []).append(op)
        for r in writes:
            self.lastw[r] = op
            self.readers[r] = []
        op.deps = [d for d in deps if not (d.eng == "pe" and eng == "pe" and not d.dma and not dma)]
        op.idx = len(self.ops[eng])
        self.ops[eng].append(op)
        return op

    def barrier(self):
        lasts = []
        for e in ENGS:
            comp = [o for o in self.ops[e] if not o.dma and o.fn is not None]
            if comp:
                lasts.append(comp[-1])
            lasts.extend(o for o in self.ops[e][self._bar_mark[e]:] if o.dma)
        for e in ENGS:
            op = Op(e, None, False)
            op.deps = list(lasts)
            op.idx = len(self.ops[e])
            self.ops[e].append(op)
        self.lastw = {}
        self.readers = {}
        self._bar_mark = {e: len(self.ops[e]) for e in ENGS}

    def emit(self):
        nc = self.nc
        for e in ENGS:
            for op in self.ops[e]:
                for d in op.deps:
                    d.signal = True
        with contextlib.ExitStack() as st:
            esem = {e: st.enter_context(nc.semaphore(f"s_{e}")) for e in ENGS}
            dsems = {e: [st.enter_context(nc.semaphore(f"d_{e}{i}")) for i in range(n)]
                     for e, n in self.n_dma_sems.items()}
            for e in ENGS:
                c = 0
                j = 0
                for op in self.ops[e]:
                    if op.dma:
                        pool = dsems[e]
                        op.dsem = pool[j % len(pool)]
                        op.dval = 16 * (j // len(pool) + 1)
                        op.dprev = 16 * (j // len(pool))
                        j += 1
                    elif op.signal:
                        c += 1
                        op.sigval = c
            block = st.enter_context(nc.Block())

            def run(e, eng):
                waited = {}

                def wait(sem, val):
                    k = id(sem)
                    if waited.get(k, 0) >= val:
                        return
                    waited[k] = val
                    eng.wait_ge(sem, val)

                for op in self.ops[e]:
                    for d in op.deps:
                        if d.dma:
                            wait(d.dsem, d.dval)
                        else:
                            wait(esem[d.eng], d.sigval)
                    if op.dma:
                        if op.dprev > 0:
                            wait(op.dsem, op.dprev)
                        op.fn(eng).then_inc(op.dsem, 16)
                    elif op.fn is None:
                        if op.signal:
                            eng.nop().then_inc(esem[e], 1)
                    else:
                        ins = op.fn(eng)
                        if op.signal:
                            ins.then_inc(esem[e], 1)

            block.tensor(lambda eng: run("pe", eng))
            block.scalar(lambda eng: run("act", eng))
            block.vector(lambda eng: run("dve", eng))
            block.gpsimd(lambda eng: run("pool", eng))
            block.sync(lambda eng: run("sp", eng))


class Arena:
    def __init__(self, ap, words):
        self.ap = ap
        self.words = words
        self.base = 0
        self.cur = 0

    def alloc(self, free_shape, dtype=F32):
        n = int(np.prod(free_shape))
        w = n if dtype == F32 else (n + 1) // 2
        w = (w + 7) // 8 * 8
        assert self.cur + w <= self.words, (self.cur, w, self.words)
        v = self.ap[:, self.cur:self.cur + w]
        self.cur += w
        if dtype != F32:
            v = v.bitcast(dtype)
        v = v[:, 0:n]
        if len(free_shape) == 2:
            v = v.rearrange("p (a b) -> p a b", a=free_shape[0])
        elif len(free_shape) == 3:
            v = v.rearrange("p (a b c) -> p a b c", a=free_shape[0], b=free_shape[1])
        return v

    def mark_persistent(self):
        self.base = self.cur

    def reset(self):
        self.cur = self.base


def seq_tiles(L, T):
    return [(0, N_META)] + [(N_META + T * i, T) for i in range((L - N_META) // T)]


class Prog:
    def __init__(self, Ls, T=256, stop_after=None, dbg=()):
        self.Ls = list(Ls)
        self.T = T
        self.offs = [int(x) for x in np.cumsum([0] + self.Ls[:-1])]
        self.Ltot = int(sum(self.Ls))
        self.stop_after = stop_after
        self.dbg = dbg
        self.uid = 0

    def k(self, *a):
        return a

    def dma(self, q, out, in_, reads=(), writes=(), **kw):
        return self.S.add(q, lambda e: e.dma_start(out=out, in_=in_, **kw), reads, writes, dma=True)

    def mm(self, out, lhsT, rhs, start, stop, reads, writes):
        return self.S.add("pe", lambda e: e.matmul(out, lhsT=lhsT, rhs=rhs, start=start, stop=stop), reads, writes)

    def tr(self, out, in_, ident, reads, writes):
        return self.S.add("pe", lambda e: e.transpose(out=out, in_=in_, identity=ident), reads, writes)

    def act(self, out, in_, func, reads, writes, eng="act", **kw):
        return self.S.add("act", lambda e: e.activation(out=out, in_=in_, func=func, **kw), reads, writes)

    def tt(self, eng, out, in0, in1, op, reads, writes):
        return self.S.add(eng, lambda e: e.tensor_tensor(out=out, in0=in0, in1=in1, op=op), reads, writes)

    def ts(self, eng, out, in0, s1, s2, op0, op1, reads, writes):
        if op1 is None:
            return self.S.add(eng, lambda e: e.tensor_scalar(out=out, in0=in0, scalar1=s1, scalar2=None, op0=op0), reads, writes)
        return self.S.add(eng, lambda e: e.tensor_scalar(out=out, in0=in0, scalar1=s1, scalar2=s2, op0=op0, op1=op1), reads, writes)

    def stt(self, eng, out, in0, scalar, in1, op0, op1, reads, writes):
        return self.S.add(eng, lambda e: e.scalar_tensor_tensor(out=out, in0=in0, scalar=scalar, in1=in1, op0=op0, op1=op1), reads, writes)

    def cp(self, eng, out, in_, reads, writes):
        if eng == "act":
            return self.S.add("act", lambda e: e.copy(out=out, in_=in_), reads, writes)
        return self.S.add(eng, lambda e: e.tensor_copy(out=out, in_=in_), reads, writes)

    def memset(self, eng, ap, val, writes):
        return self.S.add(eng, lambda e: e.memset(ap, val), (), writes)

    def psum(self):
        i = self.ps_i
        self.ps_i = (i + 1) % 8
        return self.ps[i], ("ps", i)

    def build(self):
        nc = bass.Bass("TRN2", target_bir_lowering=False)
        self.nc = nc
        Ls, Ltot = self.Ls, self.Ltot
        dr = {}

        def din(name, shape):
            dr[name] = nc.dram_tensor(name, list(shape), F32, kind="ExternalInput").ap()

        for i, L in enumerate(Ls):
            din(f"x{i}", (L - N_META, D))
        din("meta_tokens", (N_META, D))
        din("ffn1_norm", (DEPTH, D))
        din("ffn1_w_in", (DEPTH, D, 2 * D_FF))
        din("ffn1_w_out", (DEPTH, D_FF, D))
        din("mix_norm", (DEPTH, D))
        din("w_in", (DEPTH, D, IN_W))
        din("q_norm", (DEPTH, Q_LORA))
        din("w_uq", (DEPTH, Q_LORA, HEADS * 96))
        din("kv_norm", (DEPTH, KV_LORA))
        din("w_ukv", (DEPTH, KV_LORA, HEADS * 128))
        din("conv_w", (DEPTH, D_CONV, CONV_CH))
        din("conv_b", (DEPTH, CONV_CH))
        din("a_log", (DEPTH, 2, SSM_HEADS))
        din("dt_bias", (DEPTH, 2, SSM_HEADS))
        din("d_skip", (DEPTH, SSM_HEADS))
        din("ssm_norm", (DEPTH, D_INNER))
        din("w_branch", (DEPTH, 3072, D))
        din("w_out", (DEPTH, D, D))
        din("ffn2_norm", (DEPTH, D))
        din("ffn2_w_in", (DEPTH, D, 2 * D_FF))
        din("ffn2_w_out", (DEPTH, D_FF, D))
        din("final_norm", (D,))
        for i, L in enumerate(Ls):
            dr[f"y{i}"] = nc.dram_tensor(f"y{i}", [L - N_META, D], F32, kind="ExternalOutput").ap()
        self.dr = dr

        def scratch(name, shape, dtype):
            kind = "ExternalOutput" if name in self.dbg else "Internal"
            dr[name] = nc.dram_tensor(name, list(shape), dtype, kind=kind).ap()

        scratch("XT", (8, 128, Ltot), F32)
        scratch("KR", (32, Ltot), BF16)
        scratch("QT", (HEADS, 96, Ltot), BF16)
        scratch("KT", (HEADS, 64, Ltot), BF16)
        scratch("V", (Ltot, HEADS * 65), BF16)
        scratch("SZ", (Ltot, D_INNER), BF16)
        scratch("DT", (Ltot, 64), F32)
        scratch("XBC", (24, 128, Ltot), BF16)
        scratch("GT", (16, 128, Ltot), BF16)
        scratch("OA", (8, 128, Ltot), BF16)
        scratch("OM", (16, 128, Ltot), BF16)
        scratch("XS", (Ltot, D_INNER), BF16)
        scratch("BTOK", (Ltot, 512), BF16)
        scratch("BT", (4, 128, Ltot), BF16)
        scratch("CT", (4, 128, Ltot), BF16)
        scratch("YB", (Ltot, D_INNER), F32)
        self.scratch = scratch

        with contextlib.ExitStack() as st:
            AW = 51200
            arena_t = st.enter_context(nc.sbuf_tensor("arena", [128, AW], F32))
            self.A = Arena(arena_t, AW)
            self.psall = st.enter_context(nc.psum_tensor("psall", [128, 4096], F32))
            self.ps = [self.psall[:, i * 512:(i + 1) * 512] for i in range(8)]
            self.ps_i = 0
            self.S = Sched(nc)
            self.consts()
            self.S.barrier()
            self.body()
            self.S.barrier()
            self.S.emit()
        return nc

    def consts(self):
        A = self.A
        ones_f = A.alloc([128])
        self.ident = A.alloc([128])
        self.identb = A.alloc([128], BF16)
        self.onesb = A.alloc([128], BF16)
        self.memset("pool", ones_f, 1.0, ["c_onesf"])
        self.S.add("pool", lambda e: e.affine_select(out=self.ident, in_=ones_f, pattern=[[-1, 128]], compare_op=ALU.is_equal,
                                                      fill=0.0, base=0, channel_multiplier=1), ["c_onesf"], ["c_ident"])
        self.cp("dve", self.identb, self.ident, ["c_ident"], ["c_identb"])
        self.cp("dve", self.onesb, ones_f, ["c_onesf"], ["c_onesb"])
        self.ones_f = ones_f
        A.mark_persistent()

    def body(self):
        for l in range(DEPTH):
            self.ffn_phase(l, 1)
            if self.stop_after == ("ffn1", l):
                break
            self.proj1_phase(l)
            if self.stop_after == ("proj1", l):
                break
            self.proj2_phase(l)
            if self.stop_after == ("proj2", l):
                break
            self.attn_phase(l)
            if self.stop_after == ("attn", l):
                break
            self.ssd_phase(l)
            if self.stop_after == ("ssd", l):
                break
            self.merge_phase(l)
            if self.stop_after == ("merge", l):
                break
            self.ffn_phase(l, 2)
        self.out_phase()

    def load_xT(self, first, si, t0, n, xT, key, scr):
        g0 = self.offs[si] + t0
        if not first:
            self.dma("sp", xT[:, :, 0:n], self.dr["XT"][:, :, g0:g0 + n].rearrange("c p t -> p c t"),
                     [("XT", si, t0)], [key])
            return
        xin, kin = scr
        nb = max(1, n // 128)
        if t0 == 0:
            self.dma("sp", xin[0:n, 0, :], self.dr["meta_tokens"][0:n, :], (), [kin])
        else:
            src = self.dr[f"x{si}"][t0 - N_META:t0 - N_META + n, :].rearrange("(b p) d -> p b d", p=128)
            self.dma("sp", xin[:, 0:nb, :], src, (), [kin])
        pn = min(n, 128)
        for c in range(8):
            ps, pk = self.psum()
            for b in range(nb):
                self.tr(ps[:, b * 128:b * 128 + pn], xin[0:pn, b, c * 128:(c + 1) * 128], self.ident[0:pn, 0:pn],
                        [kin, "c_ident"], [pk])
            self.cp("act" if c % 2 == 0 else "dve", xT[:, c, 0:n], ps[:, 0:n], [pk], [key])

    def store_xT(self, si, t0, n, xT, kx):
        g0 = self.offs[si] + t0
        self.dma("sp", self.dr["XT"][:, :, g0:g0 + n].rearrange("c p t -> p c t"), xT[:, :, 0:n], [kx], [("XT", si, t0)])

    def tile_loop(self, tiles, load, compute):
        if tiles:
            load(0, tiles[0])
        for i, tl in enumerate(tiles):
            if i + 1 < len(tiles):
                load(i + 1, tiles[i + 1])
            compute(i, tl)

    def rms_rstd(self, src, nch, n, rstd, kr, sq, ksq, reads, dim):
        for c in range(nch):
            self.act(sq[:, c, 0:n], src[:, c, 0:n], AF.Square, reads, [ksq + (c,)])
        ps, pk = self.psum()
        for c in range(nch):
            self.mm(ps[:, 0:n], self.onesb, sq[:, c, 0:n], c == 0, c == nch - 1, [ksq + (c,), "c_onesb"], [pk])
        self.act(rstd[:, 0:n], ps[:, 0:n], AF.Sqrt, [pk], [kr], scale=1.0 / dim, bias=self.eps_col)
        self.S.add("dve", lambda e: e.reciprocal(out=rstd[:, 0:n], in_=rstd[:, 0:n]), [kr], [kr])

    def load_w_bf16(self, dst, src2d, nchunk, key, col0=0, ncol=None):
        ncol = ncol if ncol is not None else src2d.shape[1]
        for c in range(nchunk):
            self.dma("pool", dst[:, c, :], src2d[c * 128:(c + 1) * 128, col0:col0 + ncol], (), [key + (c,)],
                     max_dma_last_dim=4096)

    def load_col(self, dst, vec, nch, key):
        self.dma("sp", dst, vec.rearrange("(c p) -> p c", p=128), (), [key], allow_slow_non_contiguous=True)

    def ffn_phase(self, l, which):
        A, T = self.A, self.T
        A.reset()
        dr = self.dr
        pre = f"ffn{which}"
        first = (l == 0 and which == 1)
        last = (l == DEPTH - 1 and which == 2)
        w_in = A.alloc([8, 2 * D_FF], BF16)
        w_out = A.alloc([22, D], BF16)
        gcol = A.alloc([8])
        self.eps_col = A.alloc([1])
        self.memset("pool", self.eps_col, EPS, ["eps"])
        self.load_col(gcol, dr[f"{pre}_norm"][l], 8, ("gcol",))
        self.load_w_bf16(w_in, dr[f"{pre}_w_in"][l], 8, ("w_in",))
        self.load_w_bf16(w_out, dr[f"{pre}_w_out"][l], 22, ("w_out",))
        xTs = [A.alloc([8, T]) for _ in range(2)]
        hs = [A.alloc([8, T], BF16) for _ in range(2)]
        sq = A.alloc([8, T], BF16)
        rstd = A.alloc([T])
        aT = A.alloc([22, T], BF16)
        sg = [A.alloc([T]) for _ in range(2)]
        xin = A.alloc([T // 128, D]) if first else None
        tiles = [(si, t0, n) for si, L in enumerate(self.Ls) for (t0, n) in seq_tiles(L, T)]

        def load(i, tl):
            si, t0, n = tl
            self.load_xT(first, si, t0, n, xTs[i % 2], ("xT", i % 2), (xin, ("xin",)))

        def compute(i, tl):
            si, t0, n = tl
            s = i % 2
            xT, h = xTs[s], hs[s]
            kx = ("xT", s)
            self.rms_rstd(xT, 8, n, rstd, ("rstd",), sq, ("sq",), [kx], D)
            for c in range(8):
                self.stt("dve", h[:, c, 0:n], xT[:, c, 0:n], gcol[:, c:c + 1], rstd[:, 0:n], ALU.mult, ALU.mult,
                         [kx, ("rstd",), ("gcol",)], [("h", s, c)])
            for j in range(22):
                pg, kg = self.psum()
                pu, ku = self.psum()
                for c in range(8):
                    self.mm(pg[:, 0:n], w_in[:, c, j * 128:(j + 1) * 128], h[:, c, 0:n], c == 0, c == 7,
                            [("h", s, c), ("w_in", c)], [kg])
                for c in range(8):
                    self.mm(pu[:, 0:n], w_in[:, c, D_FF + j * 128:D_FF + (j + 1) * 128], h[:, c, 0:n], c == 0, c == 7,
                            [("h", s, c), ("w_in", c)], [ku])
                sgj = sg[j % 2]
                self.act(sgj[:, 0:n], pg[:, 0:n], AF.Silu, [kg], [("sg", j % 2)])
                self.tt("dve", aT[:, j, 0:n], sgj[:, 0:n], pu[:, 0:n], ALU.mult, [("sg", j % 2), ku], [("aT", j)])
            for oc in range(8):
                po, ko = self.psum()
                for j in range(22):
                    self.mm(po[:, 0:n], w_out[:, j, oc * 128:(oc + 1) * 128], aT[:, j, 0:n], j == 0, j == 21,
                            [("aT", j), ("w_out", j)], [ko])
                self.stt("dve", xT[:, oc, 0:n], po[:, 0:n], 0.5, xT[:, oc, 0:n], ALU.mult, ALU.add, [ko, kx], [kx])
            self.store_xT(si, t0, n, xT, kx)

        self.tile_loop(tiles, load, compute)
        self.S.barrier()

    def out_phase(self):
        A, T = self.A, self.T
        A.reset()
        dr = self.dr
        gcol = A.alloc([8])
        self.eps_col = A.alloc([1])
        self.memset("pool", self.eps_col, EPS, ["eps"])
        self.load_col(gcol, dr["final_norm"], 8, ("gcol",))
        xTs = [A.alloc([8, T]) for _ in range(2)]
        sq = A.alloc([8, T], BF16)
        rstd = A.alloc([T])
        xo = [A.alloc([T // 128, D]) for _ in range(2)]
        it = 0
        for si, L in enumerate(self.Ls):
            for (t0, n) in seq_tiles(L, T):
                if t0 == 0:
                    continue
                s = it % 2
                it += 1
                xT = xTs[s]
                kx = ("xT", s)
                self.load_xT(False, si, t0, n, xT, kx, None)
                self.rms_rstd(xT, 8, n, rstd, ("rstd",), sq, ("sq",), [kx], D)
                for c in range(8):
                    self.stt("dve", xT[:, c, 0:n], xT[:, c, 0:n], gcol[:, c:c + 1], rstd[:, 0:n], ALU.mult, ALU.mult,
                             [kx, ("rstd",), ("gcol",)], [kx])
                nb = n // 128
                for b in range(nb):
                    for cc in range(2):
                        ps, pk = self.psum()
                        for c4 in range(4):
                            c = cc * 4 + c4
                            self.tr(ps[:, c4 * 128:(c4 + 1) * 128], xT[:, c, b * 128:(b + 1) * 128], self.ident,
                                    [kx, "c_ident"], [pk])
                        self.cp("act" if cc == 0 else "dve", xo[s][:, b, cc * 512:(cc + 1) * 512], ps[:, :], [pk], [("xo", s)])
                dst = dr[f"y{si}"][t0 - N_META:t0 - N_META + n, :].rearrange("(b p) d -> p b d", p=128)
                self.dma("sp", dst, xo[s][:, 0:nb, :], [("xo", s)], [("y", si, t0)])
        self.S.barrier()


    def load_bcast(self, dst, vec, key, q="sp"):
        self.dma(q, dst, vec.partition_broadcast(128), (), [key])

    def norm_h(self, xT, n, gcol, h, s, kx):
        rstd, sq = self.rstd_t, self.sq_t
        self.rms_rstd(xT, 8, n, rstd, ("rstd",), sq, ("sq",), [kx], D)
        for c in range(8):
            self.stt("dve", h[:, c, 0:n], xT[:, c, 0:n], gcol[:, c:c + 1], rstd[:, 0:n], ALU.mult, ALU.mult,
                     [kx, ("rstd",), ("gcol",)], [("h", s, c)])

    def proj1_phase(self, l):
        A, T, dr = self.A, self.T, self.dr
        A.reset()
        NB = T // 128
        wA = A.alloc([8, 2720], BF16)
        wdt = A.alloc([8, 64], BF16)
        wkr_rot = A.alloc([8, 96], BF16)
        wuq = A.alloc([3, 16, 96], BF16)
        wuq_rot = A.alloc([3, 16, 96], BF16)
        wuk = A.alloc([2, 16, 64], BF16)
        wv = A.alloc([2, 16, 64], BF16)
        gcol = A.alloc([8])
        qg = A.alloc([3])
        kvg = A.alloc([2])
        self.eps_col = A.alloc([1])
        dtb = A.alloc([64])
        invc = A.alloc([1])
        pos0 = A.alloc([T])
        self.memset("pool", self.eps_col, EPS, ["eps"])
        self.load_col(gcol, dr["mix_norm"][l], 8, ("gcol",))
        self.load_col(qg, dr["q_norm"][l], 3, ("qg",))
        self.load_col(kvg, dr["kv_norm"][l], 2, ("kvg",))
        self.load_bcast(dtb, dr["dt_bias"][l].rearrange("a h -> (a h)"), ("dtb",))
        w2 = dr["w_in"][l]
        self.load_w_bf16(wA, w2, 8, ("wA",), 0, 2720)
        self.load_w_bf16(wdt, w2, 8, ("wdt",), O_DT, 64)
        self.memset("pool", wkr_rot[:, :, 0:64], 0.0, [("wkr_rot", c) for c in range(8)])
        for c in range(8):
            self.dma("pool", wkr_rot[:, c, 64:80], w2[c * 128:(c + 1) * 128, O_KR + 16:O_KR + 32], (), [("wkr_rot", c)])
            self.dma("pool", wkr_rot[:, c, 80:96], w2[c * 128:(c + 1) * 128, O_KR:O_KR + 16], (), [("wkr_rot", c)])
        self.ts("dve", wkr_rot[:, :, 64:80], wkr_rot[:, :, 64:80], -1.0, None, ALU.mult, None,
                [("wkr_rot", c) for c in range(8)], [("wkr_rot", c) for c in range(8)])
        wq = dr["w_uq"][l]
        for c in range(3):
            src = wq[c * 128:(c + 1) * 128, :].rearrange("p (h d) -> p h d", d=96)
            self.dma("pool", wuq[:, c, :, :], src, (), [("wuq", c)])
            self.memset("pool", wuq_rot[:, c, :, 0:64], 0.0, [("wuq_rot", c)])
            self.dma("pool", wuq_rot[:, c, :, 64:80], src[:, :, 80:96], (), [("wuq_rot", c)])
            self.dma("pool", wuq_rot[:, c, :, 80:96], src[:, :, 64:80], (), [("wuq_rot", c)])
        self.ts("dve", wuq_rot[:, :, :, 64:80], wuq_rot[:, :, :, 64:80], -1.0, None, ALU.mult, None,
                [("wuq_rot", c) for c in range(3)], [("wuq_rot", c) for c in range(3)])
        wkv = dr["w_ukv"][l]
        for c in range(2):
            src = wkv[c * 128:(c + 1) * 128, :].rearrange("p (h d) -> p h d", d=128)
            self.dma("pool", wuk[:, c, :, :], src[:, :, 0:64], (), [("wuk", c)])
            self.dma("pool", wv[:, c, :, :], src[:, :, 64:128], (), [("wv", c)])
        jcol = A.alloc([1])
        jm = A.alloc([1])
        self.S.add("pool", lambda e: e.iota(jcol[64:96, :], pattern=[[0, 1]], base=0, channel_multiplier=1,
                                            allow_small_or_imprecise_dtypes=True), (), ["jcol"])
        self.S.add("pool", lambda e: e.affine_select(out=jm[64:96, :], in_=self.ones_f[64:96, 0:1], pattern=[[0, 1]], compare_op=ALU.is_ge,
                                                      fill=0.0, base=-16, channel_multiplier=1), ["c_onesf"], ["jm"])
        self.stt("dve", jcol[64:96, :], jm[64:96, :], -16.0, jcol[64:96, :], ALU.mult, ALU.add, ["jcol", "jm"], ["jcol"])
        self.act(invc[64:96, :], jcol[64:96, :], AF.Exp, ["jcol"], ["invc"], scale=-float(np.log(10000.0) / 16.0))
        self.S.add("pool", lambda e: e.iota(pos0[64:96, :], pattern=[[1, T]], base=0, channel_multiplier=0,
                                            allow_small_or_imprecise_dtypes=True), (), ["pos0"])

        xTs = [A.alloc([8, T]) for _ in range(2)]
        hs = [A.alloc([8, T], BF16) for _ in range(2)]
        self.sq_t = A.alloc([8, T], BF16)
        self.rstd_t = A.alloc([T])
        cq = A.alloc([3, T])
        cqn = A.alloc([3, T], BF16)
        ckv = A.alloc([2, T])
        ckvn = A.alloc([2, T], BF16)
        rq = A.alloc([T])
        cs = [A.alloc([T]) for _ in range(2)]
        ang = A.alloc([T])
        angi = A.alloc([T]).bitcast(mybir.dt.int32)
        angf = A.alloc([T])
        kr_sb = A.alloc([T])
        krr_sb = A.alloc([T])
        kro = A.alloc([T], BF16)
        qo = [A.alloc([T], BF16) for _ in range(2)]
        qt1 = A.alloc([T])
        qt2 = A.alloc([T])
        ko = [A.alloc([T], BF16) for _ in range(2)]
        vo = [A.alloc([NB, 16, 65], BF16) for _ in range(2)]
        szo = [A.alloc([NB, 2048], BF16) for _ in range(2)]
        dto = [A.alloc([NB, 64]) for _ in range(2)]
        dtt = A.alloc([64])
        dtt2 = A.alloc([64])
        for s in range(2):
            self.memset("pool", vo[s][:, :, :, 64:65], 1.0, [("vo", s)])
        TWO_PI = float(2 * np.pi)
        tiles = [(si, t0, n) for si, L in enumerate(self.Ls) for (t0, n) in seq_tiles(L, T)]

        def load(i, tl):
            si, t0, n = tl
            self.load_xT(False, si, t0, n, xTs[i % 2], ("xT", i % 2), None)

        def trig(dst, n, t0, shift, key):
            self.ts("dve", ang[64:96, 0:n], pos0[64:96, 0:n], float(t0), None, ALU.add, None, ["pos0"], ["ang"])
            self.ts("dve", ang[64:96, 0:n], ang[64:96, 0:n], invc[64:96, :], None, ALU.mult, None, ["ang", "invc"], ["ang"])
            if shift != 0.0:
                self.ts("dve", ang[64:96, 0:n], ang[64:96, 0:n], shift, None, ALU.add, None, ["ang"], ["ang"])
            self.ts("dve", angf[64:96, 0:n], ang[64:96, 0:n], 1.0 / TWO_PI, None, ALU.mult, None, ["ang"], ["angf"])
            self.cp("dve", angi[64:96, 0:n], angf[64:96, 0:n], ["angf"], ["angi"])
            self.cp("dve", angf[64:96, 0:n], angi[64:96, 0:n], ["angi"], ["angf"])
            self.stt("dve", ang[64:96, 0:n], angf[64:96, 0:n], -TWO_PI, ang[64:96, 0:n], ALU.mult, ALU.add, ["angf", "ang"], ["ang"])
            self.ts("dve", ang[64:96, 0:n], ang[64:96, 0:n], float(np.pi), -float(np.pi), ALU.min, ALU.max, ["ang"], ["ang"])
            self.act(dst[64:96, 0:n], ang[64:96, 0:n], AF.Sin, ["ang"], [key])

        def small_norm(src_ps_list, raw, nch, gc, gkey, outn, okey, n, dim):
            for c, (ps_, pk_) in enumerate(src_ps_list):
                self.cp("act", raw[:, c, 0:n], ps_[:, 0:n], [pk_], [(okey, "raw", c)])
            self.rms_rstd(raw, nch, n, rq, ("rq",), self.sq_t, ("sq",), [(okey, "raw", c) for c in range(nch)], dim)
            for c in range(nch):
                self.stt("dve", outn[:, c, 0:n], raw[:, c, 0:n], gc[:, c:c + 1], rq[:, 0:n], ALU.mult, ALU.mult,
                         [(okey, "raw", c), ("rq",), gkey], [(okey, c)])

        def compute(i, tl):
            si, t0, n = tl
            s = i % 2
            g0 = self.offs[si] + t0
            xT, h = xTs[s], hs[s]
            kx = ("xT", s)
            self.norm_h(xT, n, gcol, h, s, kx)
            hk = [("h", s, c) for c in range(8)]

            def fm(wt, wkey, col0, ncol):
                ps_, pk_ = self.psum()
                for c in range(8):
                    self.mm(ps_[0:ncol, 0:n], wt[:, c, col0:col0 + ncol], h[:, c, 0:n], c == 0, c == 7, [("h", s, c), (wkey, c)], [pk_])
                return ps_, pk_

            small_norm([fm(wA, "wA", O_CQ + c * 128, 128) for c in range(3)], cq, 3, qg, ("qg",), cqn, "cqn", n, Q_LORA)
            small_norm([fm(wA, "wA", O_CKV + c * 128, 128) for c in range(2)], ckv, 2, kvg, ("kvg",), ckvn, "ckvn", n, KV_LORA)
            trig(cs[1], n, t0, 0.0, "sin")
            trig(cs[0], n, t0, float(np.pi / 2), "cos")
            pk1, kk1 = fm(wA, "wA", O_KR - 64, 96)
            pk2, kk2 = fm(wkr_rot, "wkr_rot", 0, 96)
            self.tt("dve", kr_sb[64:96, 0:n], pk1[64:96, 0:n], cs[0][64:96, 0:n], ALU.mult, [kk1, "cos"], ["kr_sb"])
            self.tt("dve", krr_sb[64:96, 0:n], pk2[64:96, 0:n], cs[1][64:96, 0:n], ALU.mult, [kk2, "sin"], ["krr_sb"])
            self.tt("dve", kro[64:96, 0:n], kr_sb[64:96, 0:n], krr_sb[64:96, 0:n], ALU.add, ["kr_sb", "krr_sb"], ["kro"])
            self.dma("sp", dr["KR"][:, g0:g0 + n], kro[64:96, 0:n], ["kro"], [("KR", g0)])
            SC = float(96.0 ** -0.5)
            for hd in range(HEADS):
                b = hd % 2
                pq, kq = self.psum()
                for c in range(3):
                    self.mm(pq[0:96, 0:n], wuq[:, c, hd, :], cqn[:, c, 0:n], c == 0, c == 2, [("cqn", c), ("wuq", c)], [kq])
                pr, krk = self.psum()
                for c in range(3):
                    self.mm(pr[0:96, 0:n], wuq_rot[:, c, hd, :], cqn[:, c, 0:n], c == 0, c == 2, [("cqn", c), ("wuq_rot", c)], [krk])
                self.act(qo[b][0:64, 0:n], pq[0:64, 0:n], AF.Copy, [kq], [("qo", b)], scale=SC)
                self.stt("dve", qt1[64:96, 0:n], pq[64:96, 0:n], SC, cs[0][64:96, 0:n], ALU.mult, ALU.mult, [kq, "cos"], ["qt1"])
                self.stt("dve", qt2[64:96, 0:n], pr[64:96, 0:n], SC, cs[1][64:96, 0:n], ALU.mult, ALU.mult, [krk, "sin"], ["qt2"])
                self.tt("dve", qo[b][64:96, 0:n], qt1[64:96, 0:n], qt2[64:96, 0:n], ALU.add, ["qt1", "qt2"], [("qo", b)])
                self.dma("sp", dr["QT"][hd, :, g0:g0 + n], qo[b][0:96, 0:n], [("qo", b)], [("QT", hd, g0)])
                pkn, kkn = self.psum()
                for c in range(2):
                    self.mm(pkn[0:64, 0:n], wuk[:, c, hd, :], ckvn[:, c, 0:n], c == 0, c == 1, [("ckvn", c), ("wuk", c)], [kkn])
                self.cp("act", ko[b][0:64, 0:n], pkn[0:64, 0:n], [kkn], [("ko", b)])
                self.dma("sp", dr["KT"][hd, :, g0:g0 + n], ko[b][0:64, 0:n], [("ko", b)], [("KT", hd, g0)])
            nb = max(1, n // 128)
            pn = min(n, 128)
            for bk in range(nb):
                tsl = slice(bk * 128, bk * 128 + pn)
                for half in range(2):
                    pv, kv = self.psum()
                    for c in range(2):
                        self.mm(pv[0:pn, :], ckvn[:, c, tsl], wv[:, c, half * 8:(half + 1) * 8, :], c == 0, c == 1,
                                [("ckvn", c), ("wv", c)], [kv])
                    self.cp("act" if half == 0 else "dve", vo[s][0:pn, bk, half * 8:(half + 1) * 8, 0:64],
                            pv[0:pn, :].rearrange("p (h d) -> p h d", d=64), [kv], [("vo", s)])
                for q4 in range(4):
                    pz, kz = self.psum()
                    for c in range(8):
                        self.mm(pz[0:pn, :], h[:, c, tsl], wA[:, c, O_Z + q4 * 512:O_Z + (q4 + 1) * 512], c == 0, c == 7,
                                [("h", s, c), ("wA", c)], [kz])
                    self.act(szo[s][0:pn, bk, q4 * 512:(q4 + 1) * 512], pz[0:pn, :], AF.Silu, [kz], [("szo", s)])
                pd, kd = self.psum()
                for c in range(8):
                    self.mm(pd[0:pn, 0:64], h[:, c, tsl], wdt[:, c, :], c == 0, c == 7, [("h", s, c), ("wdt", c)], [kd])
                self.tt("dve", dtt[0:pn, :], pd[0:pn, 0:64], dtb[0:pn, :], ALU.add, [kd, ("dtb",)], ["dtt"])
                self.act(dtt2[0:pn, :], dtt[0:pn, :], AF.Abs, ["dtt"], ["dtt2"])
                self.act(dtt2[0:pn, :], dtt2[0:pn, :], AF.Exp, ["dtt2"], ["dtt2"], scale=-1.0)
                self.act(dtt2[0:pn, :], dtt2[0:pn, :], AF.Ln, ["dtt2"], ["dtt2"], bias=1.0)
                self.stt("dve", dto[s][0:pn, bk, :], dtt[0:pn, :], 0.0, dtt2[0:pn, :], ALU.max, ALU.add, ["dtt", "dtt2"], [("dto", s)])
            if n >= 128:
                def tokv(ap3):
                    return ap3.rearrange("(b p) f -> p b f", p=128)
                self.dma("sp", tokv(dr["V"][g0:g0 + n, :]), vo[s][:, 0:nb, :, :].rearrange("p b h d -> p b (h d)"), [("vo", s)], [("V", g0)])
                self.dma("sp", tokv(dr["SZ"][g0:g0 + n, :]), szo[s][:, 0:nb, :], [("szo", s)], [("SZ", g0)])
                self.dma("sp", tokv(dr["DT"][g0:g0 + n, :]), dto[s][:, 0:nb, :], [("dto", s)], [("DT", g0)])
            else:
                self.dma("sp", dr["V"][g0:g0 + n, :], vo[s][0:n, 0, :, :].rearrange("p h d -> p (h d)"), [("vo", s)], [("V", g0)])
                self.dma("sp", dr["SZ"][g0:g0 + n, :], szo[s][0:n, 0, :], [("szo", s)], [("SZ", g0)])
                self.dma("sp", dr["DT"][g0:g0 + n, :], dto[s][0:n, 0, :], [("dto", s)], [("DT", g0)])

        self.tile_loop(tiles, load, compute)
        self.S.barrier()

    def proj2_phase(self, l):
        A, T, dr = self.A, self.T, self.dr
        A.reset()
        wB = A.alloc([8, 5120], BF16)
        gcol = A.alloc([8])
        self.eps_col = A.alloc([1])
        self.memset("pool", self.eps_col, EPS, ["eps"])
        self.load_col(gcol, dr["mix_norm"][l], 8, ("gcol",))
        w2 = dr["w_in"][l]
        for c in range(8):
            self.dma("pool", wB[:, c, 0:3072], w2[c * 128:(c + 1) * 128, O_XBC:O_XBC + 3072], (), [("wB", c)], max_dma_last_dim=4096)
            self.dma("pool", wB[:, c, 3072:5120], w2[c * 128:(c + 1) * 128, O_GATE:O_GATE + 2048], (), [("wB", c)], max_dma_last_dim=4096)
        xTs = [A.alloc([8, T]) for _ in range(2)]
        hs = [A.alloc([8, T], BF16) for _ in range(2)]
        self.sq_t = A.alloc([8, T], BF16)
        self.rstd_t = A.alloc([T])
        xo = [A.alloc([24, T], BF16) for _ in range(2)]
        go = [A.alloc([16, T], BF16) for _ in range(2)]
        tiles = [(si, t0, n) for si, L in enumerate(self.Ls) for (t0, n) in seq_tiles(L, T)]

        def load(i, tl):
            si, t0, n = tl
            self.load_xT(False, si, t0, n, xTs[i % 2], ("xT", i % 2), None)

        def compute(i, tl):
            si, t0, n = tl
            s = i % 2
            g0 = self.offs[si] + t0
            xT, h = xTs[s], hs[s]
            self.norm_h(xT, n, gcol, h, s, ("xT", s))
            for oc in range(40):
                ps_, pk_ = self.psum()
                for c in range(8):
                    self.mm(ps_[:, 0:n], wB[:, c, oc * 128:(oc + 1) * 128], h[:, c, 0:n], c == 0, c == 7, [("h", s, c), ("wB", c)], [pk_])
                if oc < 24:
                    self.cp("dve" if oc % 2 else "act", xo[s][:, oc, 0:n], ps_[:, 0:n], [pk_], [("xo", s)])
                else:
                    self.act(go[s][:, oc - 24, 0:n], ps_[:, 0:n], AF.Sigmoid, [pk_], [("go", s)])
            self.dma("sp", dr["XBC"][:, :, g0:g0 + n].rearrange("c p t -> p c t"), xo[s][:, :, 0:n], [("xo", s)], [("XBC", g0)])
            self.dma("sp", dr["GT"][:, :, g0:g0 + n].rearrange("c p t -> p c t"), go[s][:, :, 0:n], [("go", s)], [("GT", g0)])

        self.tile_loop(tiles, load, compute)
        self.S.barrier()

    def attn_phase(self, l):
        A, dr = self.A, self.dr
        A.reset()
        QT_ = 512
        Lmax = max(self.Ls)
        nkb_max = 1 + (Lmax - N_META) // 128
        Kt = [A.alloc([Lmax], BF16) for _ in range(2)]
        Vt = [A.alloc([nkb_max, 65], BF16) for _ in range(2)]
        Qt = [A.alloc([QT_], BF16) for _ in range(2)]
        Pt = [A.alloc([2, QT_], BF16) for _ in range(4)]
        den = A.alloc([QT_])
        rb = A.alloc([QT_])
        oT = [A.alloc([QT_], BF16) for _ in range(2)]
        LA = 2
        cnt = {"pi": 0, "h": 0, "q": 0, "b": 0}
        for si, L in enumerate(self.Ls):
            off = self.offs[si]
            kblocks = [(0, N_META)] + [(N_META + 128 * i, 128) for i in range((L - N_META) // 128)]
            qtiles = seq_tiles(L, QT_)
            nkb, nq = len(kblocks), len(qtiles)
            nfull = (L - N_META) // 128
            hbase, qbase = cnt["h"], cnt["q"]
            cnt["h"] += HEADS
            cnt["q"] += HEADS * nq

            def load_head(hd, off=off, L=L, nfull=nfull, hbase=hbase):
                sl = (hbase + hd) % 2
                K, V = Kt[sl], Vt[sl]
                kK, kV = ("K", sl), ("V", sl)
                self.dma("sp", K[64:96, 0:L], dr["KR"][:, off:off + L], (), [kK])
                self.dma("sp", K[0:64, 0:L], dr["KT"][hd, :, off:off + L], (), [kK])
                self.dma("sp", V[0:N_META, 0, :], dr["V"][off:off + N_META, hd * 65:(hd + 1) * 65], (), [kV])
                self.dma("sp", V[:, 1:1 + nfull, :],
                         dr["V"][off + N_META:off + L, hd * 65:(hd + 1) * 65].rearrange("(b p) d -> p b d", p=128), (), [kV])

            def load_q(hd, qi, off=off, qbase=qbase, nq=nq, qtiles=qtiles):
                qs = (qbase + hd * nq + qi) % 2
                q0, qn = qtiles[qi]
                self.dma("sp", Qt[qs][0:96, 0:qn], dr["QT"][hd, :, off + q0:off + q0 + qn], (), [("Q", qs)])

            groups = [[0]] + [[1 + 2 * j, 2 + 2 * j] for j in range(nfull // 2)] + ([[nfull]] if nfull % 2 else [])
            ng = len(groups)
            items = [(hd, qi, gi) for hd in range(HEADS) for qi in range(nq) for gi in range(ng)]
            slots = {}

            def emit_S(idx, hbase=hbase, qbase=qbase, nq=nq, groups=groups, kblocks=kblocks, qtiles=qtiles, items=items, slots=slots):
                hd, qi, gi = items[idx]
                sl = (hbase + hd) % 2
                qs = (qbase + hd * nq + qi) % 2
                q0, qn = qtiles[qi]
                grp = groups[gi]
                pr = cnt["pi"] % 3
                cnt["pi"] += 1
                bi = cnt["b"]
                cnt["b"] += 1
                P, kP = Pt[bi % 4], ("P", bi % 4)
                slots[idx] = (P, kP)
                kn = kblocks[grp[0]][1]
                for j, kb in enumerate(grp):
                    k0, kn_ = kblocks[kb]
                    assert kn_ == kn
                    bank = 2 * pr + j
                    self.mm(self.ps[bank][0:kn, 0:qn], Kt[sl][0:96, k0:k0 + kn], Qt[qs][0:96, 0:qn], True, True,
                            [("K", sl), ("Q", qs)], [("ps", bank)])
                if len(grp) == 1:
                    self.act(P[0:kn, 0, 0:qn], self.ps[2 * pr][0:kn, 0:qn], AF.Exp, [("ps", 2 * pr)], [kP])
                else:
                    src2 = self.psall[:, 2 * pr * 512:(2 * pr + 2) * 512].rearrange("p (b n) -> p b n", b=2)
                    self.act(P[0:kn, :, 0:qn], src2[0:kn, :, 0:qn], AF.Exp, [("ps", 2 * pr), ("ps", 2 * pr + 1)], [kP])

            def emit_PV(idx, off=off, hbase=hbase, qbase=qbase, nq=nq, ng=ng, nkb=nkb, groups=groups, kblocks=kblocks,
                        qtiles=qtiles, items=items, slots=slots, load_head=load_head):
                hd, qi, gi = items[idx]
                sl = (hbase + hd) % 2
                qs = (qbase + hd * nq + qi) % 2
                q0, qn = qtiles[qi]
                grp = groups[gi]
                P, kP = slots.pop(idx)
                po, ko_ = self.ps[6 + qs], ("ps", 6 + qs)
                for j, kb in enumerate(grp):
                    k0, kn = kblocks[kb]
                    self.mm(po[0:65, 0:qn], Vt[sl][0:kn, kb, :], P[0:kn, j, 0:qn], kb == 0, kb == nkb - 1, [("V", sl), kP], [ko_])
                if gi != ng - 1:
                    return
                self.cp("dve", den[64:65, 0:qn], po[64:65, 0:qn], [ko_], ["den"])
                self.S.add("dve", lambda e, qn=qn: e.reciprocal(out=den[64:65, 0:qn], in_=den[64:65, 0:qn]), ["den"], ["den"])
                pr = cnt["pi"] % 3
                cnt["pi"] += 1
                pb, kb_ = self.ps[2 * pr], ("ps", 2 * pr)
                self.mm(pb[0:64, 0:qn], self.ones_f[64:65, 0:64], den[64:65, 0:qn], True, True, ["den", "c_onesf"], [kb_])
                self.cp("dve", rb[0:64, 0:qn], pb[0:64, 0:qn], [kb_], ["rb"])
                self.tt("dve", oT[qs][0:64, 0:qn], po[0:64, 0:qn], rb[0:64, 0:qn], ALU.mult, [ko_, "rb"], [("oT", qs)])
                self.dma("pool", dr["OA"][hd // 2, (hd % 2) * 64:(hd % 2) * 64 + 64, off + q0:off + q0 + qn], oT[qs][0:64, 0:qn],
                         [("oT", qs)], [("OA", hd, off + q0)])
                if qi == nq - 1 and hd + 2 < HEADS:
                    load_head(hd + 2)

            load_head(0)
            load_head(1)
            load_q(0, 0)
            for idx in range(len(items) + LA):
                if idx < len(items):
                    hd, qi, gi = items[idx]
                    if gi == 0 and idx + ng < len(items):
                        load_q(items[idx + ng][0], items[idx + ng][1])
                    emit_S(idx)
                if idx - LA >= 0:
                    emit_PV(idx - LA)
        self.S.barrier()

    def merge_phase(self, l):
        A, T, dr = self.A, self.T, self.dr
        A.reset()
        wb = A.alloc([24, D], BF16)
        wo = A.alloc([8, D], BF16)
        self.load_w_bf16(wb, dr["w_branch"][l], 24, ("wb",))
        self.load_w_bf16(wo, dr["w_out"][l], 8, ("wo",))
        xTs = [A.alloc([8, T]) for _ in range(2)]
        oas = [A.alloc([8, T], BF16) for _ in range(2)]
        oms = [A.alloc([16, T], BF16) for _ in range(2)]
        gts = [A.alloc([16, T], BF16) for _ in range(2)]
        mix = A.alloc([8, T], BF16)
        ta = [A.alloc([T]) for _ in range(2)]
        tm = [A.alloc([T]) for _ in range(2)]
        tiles = [(si, t0, n) for si, L in enumerate(self.Ls) for (t0, n) in seq_tiles(L, T)]

        def load(i, tl):
            si, t0, n = tl
            s = i % 2
            g0 = self.offs[si] + t0
            self.load_xT(False, si, t0, n, xTs[s], ("xT", s), None)
            self.dma("sp", oas[s][:, :, 0:n], dr["OA"][:, :, g0:g0 + n].rearrange("c p t -> p c t"), [("OA",)], [("oa", s)])
            self.dma("sp", oms[s][:, :, 0:n], dr["OM"][:, :, g0:g0 + n].rearrange("c p t -> p c t"), [("OM",)], [("om", s)])
            self.dma("sp", gts[s][:, :, 0:n], dr["GT"][:, :, g0:g0 + n].rearrange("c p t -> p c t"), [("GT",)], [("gt", s)])

        def compute(i, tl):
            si, t0, n = tl
            s = i % 2
            xT = xTs[s]
            kx = ("xT", s)
            for oc in range(8):
                pa, ka = self.psum()
                for c in range(8):
                    self.mm(pa[:, 0:n], wb[:, c, oc * 128:(oc + 1) * 128], oas[s][:, c, 0:n], c == 0, c == 7, [("oa", s), ("wb", c)], [ka])
                pm, km = self.psum()
                for c in range(16):
                    self.mm(pm[:, 0:n], wb[:, 8 + c, oc * 128:(oc + 1) * 128], oms[s][:, c, 0:n], c == 0, c == 15, [("om", s), ("wb", 8 + c)], [km])
                b = oc % 2
                self.tt("dve", ta[b][:, 0:n], pa[:, 0:n], gts[s][:, oc, 0:n], ALU.mult, [ka, ("gt", s)], [("ta", b)])
                self.tt("dve", tm[b][:, 0:n], pm[:, 0:n], gts[s][:, 8 + oc, 0:n], ALU.mult, [km, ("gt", s)], [("tm", b)])
                self.tt("pool", mix[:, oc, 0:n], ta[b][:, 0:n], tm[b][:, 0:n], ALU.add, [("ta", b), ("tm", b)], [("mix", oc)])
            for oc in range(8):
                po, ko_ = self.psum()
                for c in range(8):
                    self.mm(po[:, 0:n], wo[:, c, oc * 128:(oc + 1) * 128], mix[:, c, 0:n], c == 0, c == 7, [("mix", c), ("wo", c)], [ko_])
                self.tt("dve", xT[:, oc, 0:n], po[:, 0:n], xT[:, oc, 0:n], ALU.add, [ko_, kx], [kx])
            self.store_xT(si, t0, n, xT, kx)

        self.tile_loop(tiles, load, compute)
        self.S.barrier()


    def ssd_phase(self, l):
        self.ssd_conv(l)
        self.ssd_scan(l, 1)
        self.ssd_scan(l, 0)

    def ssd_conv(self, l):
        A, T, dr = self.A, self.T, self.dr
        A.reset()
        NB = T // 128
        cw_rows = A.alloc([128])
        cb_rows = A.alloc([128])
        wcol = A.alloc([120])
        bcol = A.alloc([24])
        brow_f = A.alloc([3072])
        brow = A.alloc([3072], BF16)
        dg = A.alloc([24, 5, 128], BF16)
        self.dma("sp", cw_rows[0:120, :], dr["conv_w"][l].rearrange("j (c p) -> (j c) p", p=128), (), ["cw_rows"])
        self.dma("sp", cb_rows[0:24, :], dr["conv_b"][l].rearrange("(c p) -> c p", p=128), (), ["cb_rows"])
        self.dma("sp", brow_f[0:1, :], dr["conv_b"][l].rearrange("(o n) -> o n", o=1), (), ["brow_f"])
        self.cp("dve", brow[0:1, :], brow_f[0:1, :], ["brow_f"], ["brow"])
        ps_, pk_ = self.psum()
        self.tr(ps_[:, 0:120], cw_rows[0:120, :], self.ident[0:120, 0:120], ["cw_rows", "c_ident"], [pk_])
        self.cp("dve", wcol, ps_[:, 0:120], [pk_], ["wcol"])
        ps_, pk_ = self.psum()
        self.tr(ps_[:, 0:24], cb_rows[0:24, :], self.ident[0:24, 0:24], ["cb_rows", "c_ident"], [pk_])
        self.cp("dve", bcol, ps_[:, 0:24], [pk_], ["bcol"])
        for c in range(24):
            for j in range(5):
                self.ts("dve" if (c + j) % 2 else "pool", dg[:, c, j, :], self.identb, wcol[:, j * 24 + c:j * 24 + c + 1], None,
                        ALU.mult, None, ["c_identb", "wcol"], [("dg", c)])
        xw = [A.alloc([24, T + 4], BF16) for _ in range(2)]
        bco = [A.alloc([8, T], BF16) for _ in range(2)]
        xso = [A.alloc([NB, 2048], BF16) for _ in range(2)]
        bto = [A.alloc([NB, 512], BF16) for _ in range(2)]
        tiles = [(si, t0, n, L) for si, L in enumerate(self.Ls) for (t0, n) in seq_tiles(L, T)]

        def load(i, tl):
            si, t0, n, L = tl
            s = i % 2
            off = self.offs[si]
            lo, hi = max(t0 - 2, 0), min(t0 + n + 2, L)
            k = ("xw", s)
            if t0 - 2 < 0:
                self.memset("pool", xw[s][:, :, 0:2], 0.0, [k])
            if t0 + n + 2 > L:
                self.memset("pool", xw[s][:, :, n + 2:n + 4], 0.0, [k])
            self.dma("sp", xw[s][:, :, lo - (t0 - 2):hi - (t0 - 2)], dr["XBC"][:, :, off + lo:off + hi].rearrange("c p t -> p c t"), (), [k])

        def compute(i, tl):
            si, t0, n, L = tl
            s = i % 2
            g0 = self.offs[si] + t0
            k = ("xw", s)
            for cc in range(8):
                cidx = 16 + cc
                ps_, pk_ = self.psum()
                for j in range(5):
                    self.mm(ps_[:, 0:n], dg[:, cidx, j, :], xw[s][:, cidx, j:j + n], j == 0, j == 4, [k, ("dg", cidx)], [pk_])
                self.act(bco[s][:, cc, 0:n], ps_[:, 0:n], AF.Silu, [pk_, "bcol"], [("bco", s)], bias=bcol[:, cidx:cidx + 1])
            self.dma("sp", dr["BT"][:, :, g0:g0 + n].rearrange("c p t -> p c t"), bco[s][:, 0:4, 0:n], [("bco", s)], [("BT", g0)])
            self.dma("sp", dr["CT"][:, :, g0:g0 + n].rearrange("c p t -> p c t"), bco[s][:, 4:8, 0:n], [("bco", s)], [("CT", g0)])
            nb = max(1, n // 128)
            pn = min(n, 128)
            for bk in range(nb):
                for grp in range(5):
                    ps_, pk_ = self.psum()
                    for cc in range(4):
                        cidx = grp * 4 + cc
                        osl = ps_[0:pn, cc * 128:(cc + 1) * 128]
                        for j in range(5):
                            self.mm(osl, xw[s][:, cidx, j + bk * 128:j + bk * 128 + pn], dg[:, cidx, j, :], j == 0, False,
                                    [k, ("dg", cidx)], [pk_])
                        self.mm(osl, self.onesb[0:1, 0:pn], brow[0:1, cidx * 128:(cidx + 1) * 128], False, True, ["c_onesb", "brow"], [pk_])
                    if grp < 4:
                        self.act(xso[s][0:pn, bk, grp * 512:(grp + 1) * 512], ps_[0:pn, :], AF.Silu, [pk_], [("xso", s)])
                    else:
                        self.act(bto[s][0:pn, bk, :], ps_[0:pn, :], AF.Silu, [pk_], [("bto", s)])
            if n >= 128:
                self.dma("sp", dr["XS"][g0:g0 + n, :].rearrange("(b p) f -> p b f", p=128), xso[s][:, 0:nb, :], [("xso", s)], [("XS", g0)])
                self.dma("sp", dr["BTOK"][g0:g0 + n, :].rearrange("(b p) f -> p b f", p=128), bto[s][:, 0:nb, :], [("bto", s)], [("BTOK", g0)])
            else:
                self.dma("sp", dr["XS"][g0:g0 + n, :], xso[s][0:n, 0, :], [("xso", s)], [("XS", g0)])
                self.dma("sp", dr["BTOK"][g0:g0 + n, :], bto[s][0:n, 0, :], [("bto", s)], [("BTOK", g0)])

        self.tile_loop(tiles, load, compute)
        self.S.barrier()

    def ssd_scan(self, l, d):
        A, dr = self.A, self.dr
        A.reset()
        fwd = (d == 0)
        NEGV = -30000.0
        zeros_f = A.alloc([128])
        U = A.alloc([128], BF16)
        Uf = A.alloc([128])
        NEGf = A.alloc([128])
        NEG4 = A.alloc([4, 128], BF16)
        a_b = A.alloc([32])
        dsk = A.alloc([32])
        ssmn = A.alloc([2048])
        self.eps_col = A.alloc([1])
        self.memset("pool", self.eps_col, EPS, ["eps"])
        self.memset("pool", zeros_f, 0.0, ["zeros_f"])
        pat, cm = ([[1, 128]], -1) if fwd else ([[-1, 128]], 1)
        self.S.add("pool", lambda e: e.affine_select(out=Uf, in_=self.ones_f, pattern=pat, compare_op=ALU.is_ge, fill=0.0, base=0,
                                                      channel_multiplier=cm), ["c_onesf"], ["Uf"])
        self.S.add("pool", lambda e: e.affine_select(out=NEGf, in_=zeros_f, pattern=pat, compare_op=ALU.is_ge, fill=NEGV, base=0,
                                                      channel_multiplier=cm), ["zeros_f"], ["NEGf"])
        self.cp("dve", U, Uf, ["Uf"], ["U"])
        for r in range(4):
            self.cp("dve", NEG4[:, r, :], NEGf, ["NEGf"], ["NEG4"])
        self.load_bcast(a_b, dr["a_log"][l, d], "a_b")
        self.act(a_b, a_b, AF.Exp, ["a_b"], ["a_b"])
        self.ts("dve", a_b, a_b, -1.0, None, ALU.mult, None, ["a_b"], ["a_b"])
        if fwd:
            self.load_bcast(dsk, dr["d_skip"][l], "dsk")
            self.load_bcast(ssmn, dr["ssm_norm"][l], "ssmn")
        xs = [A.alloc([32, 64], BF16) for _ in range(2)]
        btok = [A.alloc([512], BF16) for _ in range(2)]
        BT = [A.alloc([4, 128], BF16) for _ in range(2)]
        CT = [A.alloc([4, 128], BF16) for _ in range(2)]
        dtt = [A.alloc([64]) for _ in range(2)]
        if fwd:
            ybt = [A.alloc([2048]) for _ in range(2)]
            szt = [A.alloc([2048], BF16) for _ in range(2)]
            om = A.alloc([2048])
            omT = [A.alloc([16, 128], BF16) for _ in range(2)]
            ss = A.alloc([4])
            sqj = A.alloc([512])
            t2 = A.alloc([32, 64])
        da_bf = A.alloc([32], BF16)
        W = A.alloc([32, 128], BF16)
        FT = A.alloc([64])
        negF = A.alloc([32])
        dF = A.alloc([32])
        expF = A.alloc([32])
        decs = A.alloc([32])
        cdec = A.alloc([32])
        dtd = A.alloc([32])
        xr = A.alloc([32, 64], BF16)
        xdec = A.alloc([32, 64], BF16)
        run = A.alloc([32, 64])
        prev = A.alloc([2048], BF16)
        CBs = A.alloc([4, 128])
        LT = [A.alloc([8, 128]) for _ in range(2)]
        MT = [A.alloc([8, 128], BF16) for _ in range(2)]
        t1 = [A.alloc([8, 64]) for _ in range(2)]
        yt = [A.alloc([2048]) for _ in range(2)]

        chunks = []
        for si, L in enumerate(self.Ls):
            cl = [(si, 0, N_META)] + [(si, N_META + 128 * i, 128) for i in range((L - N_META) // 128)]
            if not fwd:
                cl = cl[::-1]
            cl = [c + (j == 0,) for j, c in enumerate(cl)]
            chunks.extend(cl)

        def load(i, ch):
            si, t0, kn, firstc = ch
            s = i % 2
            g0 = self.offs[si] + t0
            self.dma("sp", xs[s][0:kn, :, :].rearrange("p h d -> p (h d)"), dr["XS"][g0:g0 + kn, :], (), [("xs", s)])
            self.dma("sp", btok[s][0:kn, :], dr["BTOK"][g0:g0 + kn, :], (), [("btok", s)])
            self.dma("sp", BT[s][:, :, 0:kn], dr["BT"][:, :, g0:g0 + kn].rearrange("c p t -> p c t"), (), [("BT", s)])
            self.dma("sp", CT[s][:, :, 0:kn], dr["CT"][:, :, g0:g0 + kn].rearrange("c p t -> p c t"), (), [("CT", s)])
            self.dma("sp", dtt[s][0:kn, :], dr["DT"][g0:g0 + kn, :], (), [("dt", s)])
            if fwd:
                self.dma("sp", ybt[s][0:kn, :], dr["YB"][g0:g0 + kn, :], (), [("yb", s)])
                self.dma("sp", szt[s][0:kn, :], dr["SZ"][g0:g0 + kn, :], (), [("sz", s)])

        def compute(i, ch):
            si, t0, kn, firstc = ch
            s = i % 2
            g0 = self.offs[si] + t0
            if firstc:
                self.memset("pool", run, 0.0, [("run", g) for g in range(4)])
            dt_d = dtt[s][0:kn, d * 32:(d + 1) * 32]
            kdt = ("dt", s)
            self.tt("dve", da_bf[0:kn, :], dt_d, a_b[0:kn, :], ALU.mult, [kdt, "a_b"], ["da_bf"])
            self.tt("pool", W[0:kn, :, 0:kn], da_bf[0:kn, :].unsqueeze(2).to_broadcast([kn, 32, kn]),
                    U[0:kn, 0:kn].unsqueeze(1).to_broadcast([kn, 32, kn]), ALU.mult, ["da_bf", "U"], ["W"])
            psF, kF = self.psum()
            self.mm(psF[0:kn, 0:32], U[0:kn, 0:kn], da_bf[0:kn, :], True, True, ["U", "da_bf"], [kF])
            self.mm(psF[:, 32:64], self.onesb[0:kn, :], da_bf[0:kn, :], True, True, ["c_onesb", "da_bf"], [kF])
            self.cp("dve", FT[0:kn, 0:32], psF[0:kn, 0:32], [kF], ["FTa"])
            self.cp("dve", FT[:, 32:64], psF[:, 32:64], [kF], ["FTb"])
            self.ts("dve", negF[0:kn, :], FT[0:kn, 0:32], -1.0, None, ALU.mult, None, ["FTa"], ["negF"])
            self.tt("dve", dF[0:kn, :], FT[0:kn, 32:64], FT[0:kn, 0:32], ALU.subtract, ["FTa", "FTb"], ["dF"])
            self.act(expF[0:kn, :], FT[0:kn, 0:32], AF.Exp, ["FTa"], ["expF"])
            self.act(decs[0:kn, :], dF[0:kn, :], AF.Exp, ["dF"], ["decs"])
            self.act(cdec, FT[:, 32:64], AF.Exp, ["FTb"], ["cdec"])
            self.tt("dve", dtd[0:kn, :], dt_d, decs[0:kn, :], ALU.mult, [kdt, "decs"], ["dtd"])
            kxs = ("xs", s)
            self.tt("pool", xr[0:kn, :, :], xs[s][0:kn, :, :], dt_d.unsqueeze(2).to_broadcast([kn, 32, 64]), ALU.mult, [kxs, kdt], ["xr"])
            self.tt("pool", xdec[0:kn, :, :], xs[s][0:kn, :, :], dtd[0:kn, :].unsqueeze(2).to_broadcast([kn, 32, 64]), ALU.mult,
                    [kxs, "dtd"], ["xdec"])
            self.cp("act", prev, run.rearrange("p h d -> p (h d)"), [("run", g) for g in range(4)], ["prev"])
            psCB, kCB = self.psum()
            for g in range(4):
                self.mm(psCB[0:kn, g * 128:g * 128 + kn], BT[s][:, g, 0:kn], CT[s][:, g, 0:kn], True, True, [("BT", s), ("CT", s)], [kCB])
            self.cp("act", CBs[0:kn, :, :], psCB[0:kn, :].rearrange("p (g l) -> p g l", g=4), [kCB], ["CBs"])
            y = yt[s]
            for g in range(4):
                b = g % 2
                for half in range(2):
                    ps_, pk_ = self.psum()
                    h0 = g * 8 + half * 4
                    self.mm(ps_[0:kn, 0:4 * kn], self.onesb[0:kn, 0:kn], W[0:kn, h0:h0 + 4, 0:kn], True, False, ["c_onesb", "W"], [pk_])
                    self.mm(ps_[0:kn, 0:4 * kn], self.identb[0:kn, 0:kn], NEG4[0:kn, :, 0:kn], False, True, ["c_identb", "NEG4"], [pk_])
                    for hh in range(4):
                        self.act(LT[b][0:kn, half * 4 + hh, 0:kn], ps_[0:kn, hh * kn:(hh + 1) * kn], AF.Exp, [pk_, "negF"], [("LT", b)],
                                 bias=negF[0:kn, h0 + hh:h0 + hh + 1])
                self.tt("dve", MT[b][0:kn, :, 0:kn], LT[b][0:kn, :, 0:kn], CBs[0:kn, g:g + 1, 0:kn].to_broadcast([kn, 8, kn]), ALU.mult,
                        [("LT", b), "CBs"], [("MT", b)])
                psY, kY = self.psum()
                for hh in range(8):
                    self.mm(psY[0:kn, hh * 64:(hh + 1) * 64], MT[b][0:kn, hh, 0:kn], xr[0:kn, g * 8 + hh, :], True, True, [("MT", b), "xr"], [kY])
                psO, kO = self.psum()
                self.mm(psO[0:kn, :], CT[s][:, g, 0:kn], prev[:, g * 512:(g + 1) * 512], True, True, [("CT", s), "prev"], [kO])
                self.tt("dve", t1[b][0:kn, :, :], psO[0:kn, :].rearrange("p (h d) -> p h d", d=64),
                        expF[0:kn, g * 8:(g + 1) * 8].unsqueeze(2).to_broadcast([kn, 8, 64]), ALU.mult, [kO, "expF"], [("t1", b)])
                self.tt("dve", y[0:kn, g * 512:(g + 1) * 512], psY[0:kn, :], t1[b][0:kn, :, :].rearrange("p h d -> p (h d)"), ALU.add,
                        [kY, ("t1", b)], [("y", s)])
                psS, kS = self.psum()
                self.mm(psS[:, :], btok[s][0:kn, g * 128:(g + 1) * 128], xdec[0:kn, g * 8:(g + 1) * 8, :], True, True, [("btok", s), "xdec"], [kS])
                rg = run[:, g * 8:(g + 1) * 8, :]
                self.tt("pool", rg, rg, cdec[:, g * 8:(g + 1) * 8].unsqueeze(2).to_broadcast([128, 8, 64]), ALU.mult,
                        [("run", g), "cdec", "prev"], [("run", g)])
                self.tt("dve", rg, rg, psS[:, :].rearrange("p (h d) -> p h d", d=64), ALU.add, [("run", g), kS], [("run", g)])
            if not fwd:
                self.dma("sp", dr["YB"][g0:g0 + kn, :], y[0:kn, :], [("y", s)], [("YB", g0)])
                return
            ky = ("y", s)
            self.tt("dve", y[0:kn, :], y[0:kn, :], ybt[s][0:kn, :], ALU.add, [ky, ("yb", s)], [ky])
            self.tt("pool", t2[0:kn, :, :], xs[s][0:kn, :, :], dsk[0:kn, :].unsqueeze(2).to_broadcast([kn, 32, 64]), ALU.mult, [kxs, "dsk"], ["t2"])
            self.tt("pool", y[0:kn, :], y[0:kn, :], t2[0:kn, :, :].rearrange("p h d -> p (h d)"), ALU.add, [ky, "t2"], [ky])
            self.tt("dve", y[0:kn, :], y[0:kn, :], szt[s][0:kn, :], ALU.mult, [ky, ("sz", s)], [ky])
            self.memset("pool", ss[0:kn, :], 0.0, ["ss"])
            for g in range(4):
                self.act(sqj[0:kn, :], y[0:kn, g * 512:(g + 1) * 512], AF.Square, [ky, "ss"], ["sqj", "ss"], accum_out=ss[0:kn, g:g + 1])
            self.act(ss[0:kn, :], ss[0:kn, :], AF.Sqrt, ["ss", "eps"], ["ss"], scale=1.0 / 512, bias=self.eps_col[0:kn, :])
            self.S.add("dve", lambda e: e.reciprocal(out=ss[0:kn, :], in_=ss[0:kn, :]), ["ss"], ["ss"])
            self.tt("dve", y[0:kn, :].rearrange("p (g f) -> p g f", g=4), y[0:kn, :].rearrange("p (g f) -> p g f", g=4),
                    ss[0:kn, :].unsqueeze(2).to_broadcast([kn, 4, 512]), ALU.mult, [ky, "ss"], [ky])
            self.tt("pool", om[0:kn, :], y[0:kn, :], ssmn[0:kn, :], ALU.mult, [ky, "ssmn"], ["om"])
            for q4 in range(4):
                ps_, pk_ = self.psum()
                for cc in range(4):
                    c = q4 * 4 + cc
                    self.tr(ps_[:, cc * 128:cc * 128 + kn], om[0:kn, c * 128:(c + 1) * 128], self.ident[0:kn, 0:kn], ["om", "c_ident"], [pk_])
                self.cp("act" if q4 % 2 else "dve", omT[s][:, q4 * 4:(q4 + 1) * 4, 0:kn],
                        ps_[:, :].rearrange("p (c t) -> p c t", c=4)[:, :, 0:kn], [pk_], [("omT", s)])
            self.dma("sp", dr["OM"][:, :, g0:g0 + kn].rearrange("c p t -> p c t"), omT[s][:, :, 0:kn], [("omT", s)], [("OM", g0)])

        self.tile_loop(chunks, load, compute)
        self.S.barrier()


_CACHE = {}


def _get_prog(Ls):
    key = tuple(Ls)
    if key not in _CACHE:
        p = Prog(Ls)
        _CACHE[key] = (p, p.build())
    return _CACHE[key]


WEIGHT_NAMES = ["meta_tokens", "ffn1_norm", "ffn1_w_in", "ffn1_w_out", "mix_norm", "w_in", "q_norm", "w_uq", "kv_norm",
                "w_ukv", "conv_w", "conv_b", "a_log", "dt_bias", "d_skip", "ssm_norm", "w_branch", "w_out", "ffn2_norm",
                "ffn2_w_in", "ffn2_w_out", "final_norm"]


def kernel(**inputs):
    xp = np.asarray(inputs["x_prompt"], dtype=np.float32)
    xs = np.asarray(inputs["x_sample"], dtype=np.float32)
    Ls = (xp.shape[1] + N_META, xs.shape[1] + N_META)
    prog, nc = _get_prog(Ls)
    w = {k: np.ascontiguousarray(np.asarray(inputs[k], dtype=np.float32)) for k in WEIGHT_NAMES}
    in_maps = []
    for c in range(8):
        m = dict(w)
        m["x0"] = np.ascontiguousarray(xp[c])
        m["x1"] = np.ascontiguousarray(xs[c % 4])
        in_maps.append(m)
    res = run_bass_kernel_spmd(nc, in_maps, core_ids=list(range(8)))
    y_prompt = np.stack([res.results[c]["y0"] for c in range(8)], axis=0)
    y_sample = np.stack([res.results[c]["y1"] for c in range(4)], axis=0)
    return (y_prompt, y_sample)
```
